# Optimizing a Trainium2 kernel written in Bass

```python
import jax, jax.numpy as jnp
from jax import lax
import numpy as np

D_MODEL = 1024
BATCH = 16
SEQ = 2048
DEPTH = 2
DEC_BATCH = 128
DEC_SEQ = 8
PAST_LEN = 16384
PAGE_SIZE = 128

N_HEADS = 8
N_KV_HEADS = 2
HEAD_DIM = 64
GQA_GROUP = N_HEADS // N_KV_HEADS
ATTN_WIDTH = N_HEADS * HEAD_DIM
KV_WIDTH = N_KV_HEADS * HEAD_DIM
POOL_WINDOWS = (2, 4, 8, 16)
POOL_WIDTH = D_MODEL - ATTN_WIDTH
POOL_GROUP = POOL_WIDTH // len(POOL_WINDOWS)
POOL_HIST = max(POOL_WINDOWS) - 1
MIX_WIDTH = ATTN_WIDTH + POOL_WIDTH
IN_WIDTH = ATTN_WIDTH + 2 * KV_WIDTH + POOL_WIDTH
WINDOW = 128
BLOCK = WINDOW
ROT_DIM = HEAD_DIM // 4
ROPE_THETA = 500000.0
D_FF = 4 * D_MODEL
N_META = 16
RMS_EPS = 1e-5

kernel_name = "hymba_pool_swa_sink_hybrid_step"


def _rmsnorm(x, g):
    xf = x.astype(jnp.float32)
    r = lax.rsqrt(jnp.mean(xf * xf, axis=-1, keepdims=True) + RMS_EPS)
    return (xf * r * g.astype(jnp.float32)).astype(x.dtype)


def _project(h, g, w):
    xn = _rmsnorm(h, g)
    p = jnp.einsum('btd,de->bte', xn, w)
    b, t = h.shape[0], h.shape[1]
    q = p[..., :ATTN_WIDTH].reshape(b, t, N_HEADS, HEAD_DIM)
    k = p[..., ATTN_WIDTH:ATTN_WIDTH + KV_WIDTH].reshape(b, t, N_KV_HEADS, HEAD_DIM)
    v = p[..., ATTN_WIDTH + KV_WIDTH:ATTN_WIDTH + 2 * KV_WIDTH].reshape(b, t, N_KV_HEADS, HEAD_DIM)
    u = p[..., ATTN_WIDTH + 2 * KV_WIDTH:]
    return q, k, v, u


def _rope(x, pos):
    inv_freq = ROPE_THETA ** (-jnp.arange(0, ROT_DIM, 2, dtype=jnp.float32) / ROT_DIM)
    ang = pos.astype(jnp.float32)[:, None] * inv_freq[None, :]
    cos = jnp.cos(ang)[None, :, None, :]
    sin = jnp.sin(ang)[None, :, None, :]
    xr = x[..., :ROT_DIM].astype(jnp.float32)
    x1, x2 = xr[..., :ROT_DIM // 2], xr[..., ROT_DIM // 2:]
    rot = jnp.concatenate([x1 * cos - x2 * sin, x2 * cos + x1 * sin], axis=-1)
    return jnp.concatenate([rot.astype(x.dtype), x[..., ROT_DIM:]], axis=-1)


def _band_attention(q, k, v, qpos, kpos, sink):
    scale = HEAD_DIM ** -0.5
    s = jnp.einsum('bnqkgd,bnskd->bnkgqs', q, k).astype(jnp.float32) * scale
    dpos = qpos[:, :, None] - kpos[:, None, :]
    mask = (kpos[:, None, :] >= 0) & (dpos >= 0) & (dpos <= WINDOW)
    s = jnp.where(mask[None, :, None, None], s, -jnp.inf)
    sk = sink.astype(jnp.float32).reshape(N_KV_HEADS, GQA_GROUP)[None, None, :, :, None, None]
    m = jnp.maximum(jnp.max(s, axis=-1, keepdims=True), sk)
    p = jnp.exp(s - m)
    denom = jnp.sum(p, axis=-1, keepdims=True) + jnp.exp(sk - m)
    return jnp.einsum('bnkgqs,bnskd->bnqkgd', (p / denom).astype(v.dtype), v)


def _prompt_attention(q, k, v, sink):
    b, L = q.shape[0], q.shape[1]
    padf = BLOCK - N_META
    lp = L + padf
    nb = lp // BLOCK
    pad = lambda a: jnp.pad(a, ((0, 0), (padf, 0)) + ((0, 0),) * (a.ndim - 2))
    qb = pad(q).reshape(b, nb, BLOCK, N_KV_HEADS, GQA_GROUP, HEAD_DIM)
    kb = pad(k).reshape(b, nb, BLOCK, N_KV_HEADS, HEAD_DIM)
    vb = pad(v).reshape(b, nb, BLOCK, N_KV_HEADS, HEAD_DIM)
    shift = lambda a: jnp.concatenate([jnp.zeros_like(a[:, :1]), a[:, :-1]], axis=1)
    kk = jnp.concatenate([shift(kb), kb], axis=2)
    vv = jnp.concatenate([shift(vb), vb], axis=2)
    posb = (jnp.arange(lp) - padf).reshape(nb, BLOCK)
    kpos = jnp.concatenate([posb - BLOCK, posb], axis=1)
    o = _band_attention(qb, kk, vv, posb, kpos, sink)
    return o.reshape(b, lp, ATTN_WIDTH)[:, padf:]


def _sample_attention(q, k, v, ck, cv, sink):
    b, s = q.shape[0], q.shape[1]
    qb = q.reshape(b, 1, s, N_KV_HEADS, GQA_GROUP, HEAD_DIM)
    kk = jnp.concatenate([ck, k], axis=1)[:, None]
    vv = jnp.concatenate([cv, v], axis=1)[:, None]
    qpos = (PAST_LEN + jnp.arange(s))[None]
    kpos = (PAST_LEN - WINDOW + jnp.arange(WINDOW + s))[None]
    o = _band_attention(qb, kk, vv, qpos, kpos, sink)
    return o.reshape(b, s, ATTN_WIDTH)


def _pool_mix(u_all, pos_all, n_out, pool_w, pool_scale):
    T = u_all.shape[1]
    uf = u_all.astype(jnp.float32)
    cs = jnp.cumsum(uf, axis=1)
    cs0 = jnp.concatenate([jnp.zeros_like(cs[:, :1]), cs], axis=1)
    t_idx = jnp.arange(T - n_out, T)
    outs = []
    for g, w in enumerate(POOL_WINDOWS):
        lo_c, hi_c = g * POOL_GROUP, (g + 1) * POOL_GROUP
        lo = jnp.maximum(t_idx + 1 - w, 0)
        wsum = cs0[:, t_idx + 1, lo_c:hi_c] - cs0[:, lo, lo_c:hi_c]
        cnt = jnp.minimum(pos_all[t_idx] + 1, w).astype(jnp.float32)
        d = wsum / cnt[None, :, None] - uf[:, t_idx, lo_c:hi_c]
        outs.append(jnp.einsum('btc,ce->bte', d, pool_w[g].astype(jnp.float32)))
    out = jnp.concatenate(outs, axis=-1) * pool_scale.astype(jnp.float32)
    return out.astype(u_all.dtype)


def _mlp(h, g, w_up, w_down):
    xn = _rmsnorm(h, g)
    a = jax.nn.relu(jnp.einsum('btd,df->btf', xn, w_up))
    return jnp.einsum('btf,fd->btd', a * a, w_down)


def setup_inputs(seed: int = 0) -> dict:
    key = jax.random.key(seed)
    ks = jax.random.split(key, 17)
    n = lambda k, s, sc=1.0: jax.random.normal(k, s, jnp.float32) * sc
    return {
        "x_prompt": n(ks[0], (BATCH, SEQ, D_MODEL)),
        "x_sample": n(ks[1], (DEC_BATCH, DEC_SEQ, D_MODEL)),
        "cache_k": n(ks[2], (DEPTH, DEC_BATCH, WINDOW, N_KV_HEADS, HEAD_DIM)),
        "cache_v": n(ks[3], (DEPTH, DEC_BATCH, WINDOW, N_KV_HEADS, HEAD_DIM)),
        "state_pool": n(ks[4], (DEPTH, DEC_BATCH, POOL_HIST, POOL_WIDTH)),
        "meta_tokens": n(ks[5], (N_META, D_MODEL)),
        "ln1": 1.0 + n(ks[6], (DEPTH, D_MODEL), 0.02),
        "w_in": n(ks[7], (DEPTH, D_MODEL, IN_WIDTH), D_MODEL ** -0.5),
        "attn_sinks": n(ks[8], (DEPTH, N_HEADS), 0.5),
        "pool_w": n(ks[9], (DEPTH, len(POOL_WINDOWS), POOL_GROUP, POOL_GROUP), POOL_GROUP ** -0.5),
        "pool_scale": 1.0 + n(ks[10], (DEPTH, POOL_WIDTH), 0.1),
        "w_out": n(ks[11], (DEPTH, MIX_WIDTH, D_MODEL), MIX_WIDTH ** -0.5),
        "ln2": 1.0 + n(ks[12], (DEPTH, D_MODEL), 0.02),
        "w_up": n(ks[13], (DEPTH, D_MODEL, D_FF), D_MODEL ** -0.5),
        "w_down": n(ks[14], (DEPTH, D_FF, D_MODEL), D_FF ** -0.5),
        "ln_f": 1.0 + n(ks[15], (D_MODEL,), 0.02),
    }


def reference(x_prompt, x_sample, cache_k, cache_v, state_pool, meta_tokens, ln1, w_in, attn_sinks,
              pool_w, pool_scale, w_out, ln2, w_up, w_down, ln_f):
    b = x_prompt.shape[0]
    meta = jnp.broadcast_to(meta_tokens.astype(x_prompt.dtype)[None], (b, N_META, D_MODEL))
    hp = jnp.concatenate([meta, x_prompt], axis=1)
    L = hp.shape[1]
    pos_p = jnp.arange(L)
    hs = x_sample
    s = x_sample.shape[1]
    pos_s = PAST_LEN + jnp.arange(s)
    pos_pool_s = PAST_LEN - POOL_HIST + jnp.arange(POOL_HIST + s)
    pk, pv, pu, sk, sv, su = [], [], [], [], [], []
    for l in range(DEPTH):
        q, k, v, u = _project(hp, ln1[l], w_in[l])
        q = _rope(q, pos_p)
        k = _rope(k, pos_p)
        a = _prompt_attention(q, k, v, attn_sinks[l])
        pm = _pool_mix(u, pos_p, L, pool_w[l], pool_scale[l])
        hp = hp + jnp.einsum('bte,ed->btd', jnp.concatenate([a, pm], axis=-1), w_out[l])
        hp = hp + _mlp(hp, ln2[l], w_up[l], w_down[l])
        pk.append(k[:, -WINDOW:])
        pv.append(v[:, -WINDOW:])
        pu.append(u[:, -POOL_HIST:])
        q, k, v, u = _project(hs, ln1[l], w_in[l])
        q = _rope(q, pos_s)
        k = _rope(k, pos_s)
        a = _sample_attention(q, k, v, cache_k[l], cache_v[l], attn_sinks[l])
        u_all = jnp.concatenate([state_pool[l].astype(u.dtype), u], axis=1)
        pm = _pool_mix(u_all, pos_pool_s, s, pool_w[l], pool_scale[l])
        hs = hs + jnp.einsum('bte,ed->btd', jnp.concatenate([a, pm], axis=-1), w_out[l])
        hs = hs + _mlp(hs, ln2[l], w_up[l], w_down[l])
        sk.append(jnp.concatenate([cache_k[l].astype(k.dtype), k], axis=1)[:, -WINDOW:])
        sv.append(jnp.concatenate([cache_v[l].astype(v.dtype), v], axis=1)[:, -WINDOW:])
        su.append(u_all[:, -POOL_HIST:])
    y_prompt = _rmsnorm(hp, ln_f)[:, N_META:]
    y_sample = _rmsnorm(hs, ln_f)
    return (y_prompt, y_sample, jnp.stack(pk), jnp.stack(pv), jnp.stack(pu), jnp.stack(sk), jnp.stack(sv), jnp.stack(su))
```

```python
import contextlib
import numpy as np
import ml_dtypes
import concourse.bass as bass
import concourse.mybir as mybir
from concourse.bass_utils import run_bass_kernel_spmd

F32 = mybir.dt.float32
BF16 = mybir.dt.bfloat16
AF = mybir.ActivationFunctionType
ALU = mybir.AluOpType
AX = mybir.AxisListType

D = 1024
L = 2
NCORES = 8
SEQ = 2048
PAST = 16384
NEG = -30000.0
SCALE = 0.125
EPS = 1e-5
WINS = (2, 4, 8, 16)
NT = 34
MTS = [list(range(0, 7)), list(range(7, 14)), list(range(14, 21)), list(range(21, 28)), list(range(28, 34))]
NPIECE = 8


class Buf:
    def __init__(self, name):
        self.name = name
        self.writers = []
        self.gen_base = []
        self.readers = []
        self.sem = None
        self.cnt = 0


class Op:
    __slots__ = ("eng", "fn", "deps", "dma", "ndma", "sembuf", "tok", "epoch", "has_dep", "idx", "cost", "start", "end",
                 "in_deps", "out_deps", "bank", "ndummy")


DEFAULT_COST = {"pe": 0.6, "act": 0.5, "dve": 0.5, "pool": 0.3, "sp": 0.1}


class Sched:
    def __init__(self):
        self.ops = []
        self.epoch = 0

    def op(self, eng, fn, reads=(), writes=(), pwrites=(), dma=0, sembuf=None, cost=None, nbytes=0):
        o = Op()
        o.eng, o.fn, o.dma, o.ndma, o.sembuf = eng, fn, bool(dma), dma, sembuf
        o.epoch = self.epoch
        o.has_dep = False
        o.tok = None
        o.idx = len(self.ops)
        o.cost = (cost if cost is not None else DEFAULT_COST[eng], nbytes)
        deps = set()
        in_deps = set()
        o.bank = None
        o.ndummy = 0
        for b in reads:
            if not getattr(b, "excl", False):
                in_deps.update(b.writers)
        for b in list(writes) + list(pwrites):
            if getattr(b, "excl", False) and o.bank is None:
                o.bank = b.bank
        o.in_deps = in_deps
        for b in reads:
            deps.update(b.writers)
            if getattr(b, "excl", False):
                for r_ in b.readers:
                    if self.ops[r_].eng != eng:
                        deps.add(r_)
        out_deps = set()
        for b in writes:
            out_deps.update(b.writers)
            out_deps.update(b.readers)
            out_deps.update(b.gen_base)
        for b in pwrites:
            if b.readers:
                b.gen_base = list(b.readers)
                b.writers = []
                b.readers = []
            out_deps.update(b.gen_base)
            if getattr(b, "excl", False):
                for w_ in b.writers:
                    if self.ops[w_].eng != eng:
                        out_deps.add(w_)
        out_deps.discard(o.idx)
        o.out_deps = out_deps
        deps.update(out_deps)
        deps.discard(o.idx)
        o.deps = sorted(deps)
        self.ops.append(o)
        for b in reads:
            b.readers.append(o.idx)
        for b in writes:
            b.writers = [o.idx]
            b.gen_base = [o.idx]
            b.readers = []
        for b in pwrites:
            b.writers.append(o.idx)
        return o.idx


def _skip_edge(od, o):
    return od.eng == "pe" and o.eng == "pe" and not od.dma and not o.dma


ENGS = ("pe", "act", "dve", "pool", "sp")
DMA_BW = 170e3


PRIO_MODE = "cp"
SLACK = 0.0
PERT_AMP = 0.0
PERT_SEED = 0
HOP_LAT = 0.0
SELF_LAT = 0.0


def list_schedule(ops):
    n = len(ops)
    succ = [[] for _ in range(n)]
    indeg = [0] * n
    for o in ops:
        indeg[o.idx] = len(o.deps)
        for d in o.deps:
            succ[d].append(o.idx)
    cp = [0.0] * n
    for i in range(n - 1, -1, -1):
        o = ops[i]
        c = o.cost[0] + (2.0 + o.cost[1] / DMA_BW if o.dma else 0.0)
        m = 0.0
        for s_ in succ[i]:
            if cp[s_] > m:
                m = cp[s_]
        cp[i] = c + m
    if PERT_AMP > 0.0:
        rs = np.random.RandomState(PERT_SEED)
        pert = 1.0 + PERT_AMP * (rs.rand(n) - 0.5)
        cp = [c_ * p_ for c_, p_ in zip(cp, pert)]
    ready_t = [0.0] * n
    pend = {e: set() for e in ENGS}
    for o in ops:
        if indeg[o.idx] == 0:
            pend[o.eng].add(o.idx)
    eng_free = {e: 0.0 for e in ENGS}
    dma_free = [0.0]
    order = {e: [] for e in ENGS}
    done = 0
    while done < n:
        best = None
        for e in ENGS:
            if not pend[e]:
                continue
            tfree = eng_free[e]
            cand = None
            for idx in pend[e]:
                st = max(tfree, ready_t[idx])
                if PRIO_MODE == "cp":
                    key = (max(st, tfree + SLACK), -cp[idx], idx, st)
                else:
                    key = (st, idx)
                if cand is None or key < cand:
                    cand = key
            k2 = (cand[-1], cand[2]) if PRIO_MODE == "cp" else (cand[0], cand[-1])
            if best is None or k2 < best[0]:
                best = (k2, e)
        (st, idx), e = best
        pend[e].discard(idx)
        o = ops[idx]
        c, nb = o.cost
        if o.dma:
            issue_end = st + c
            xfer_start = max(issue_end, dma_free[0])
            xfer = nb / DMA_BW
            dma_free[0] = xfer_start + xfer
            fin = xfer_start + xfer + 2.0
            eng_free[e] = issue_end
        else:
            fin = st + c
            eng_free[e] = fin
        o.start, o.end = st, fin
        order[e].append(idx)
        done += 1
        for s_ in succ[idx]:
            f2 = fin + (HOP_LAT if ops[s_].eng != e else SELF_LAT)
            if f2 > ready_t[s_]:
                ready_t[s_] = f2
            indeg[s_] -= 1
            if indeg[s_] == 0:
                pend[ops[s_].eng].add(s_)
    return order


DUMMY_MIN_GAP = 0.4
DUMMY_FILL = 0.6
DUMMY_MAX = 2


def lower(nc, sched, stack, final_bufs, reorder=True, dummy=None, dummy_deps=()):
    ops = sched.ops
    for o in ops:
        for d in o.deps:
            od = ops[d]
            if _skip_edge(od, o):
                continue
            od.has_dep = True
    for d in dummy_deps:
        ops[d].has_dep = True
    if reorder:
        order = list_schedule(ops)
    else:
        order = {e: [o.idx for o in ops if o.eng == e] for e in ENGS}
    esem = {}
    ecnt = {}
    nsem = [0]

    def newsem(name):
        nsem[0] += 1
        return stack.enter_context(nc.semaphore(name))

    for e in ENGS:
        for idx in order[e]:
            o = ops[idx]
            if o.dma:
                b = o.sembuf
                if b.sem is None:
                    b.sem = newsem("d_" + b.name)
                b.cnt += 16 * o.ndma
                o.tok = (b.sem, b.cnt)
            elif o.has_dep:
                key = (o.eng, o.epoch)
                if key not in esem:
                    esem[key] = newsem("e_%s_%d" % key)
                    ecnt[key] = 0
                ecnt[key] += 1
                o.tok = (esem[key], ecnt[key])
    finals = [(b.sem, b.cnt) for b in final_bufs if b.sem is not None]

    if dummy is not None:
        prev_end = 0.0
        for idx in order["pe"]:
            o = ops[idx]
            gap = o.start - prev_end
            prev_end = o.end
            if o.bank is not None and gap > DUMMY_MIN_GAP and len(o.in_deps - o.out_deps) > 0:
                o.ndummy = min(DUMMY_MAX, int(gap * DUMMY_FILL / 0.22))

    def emit(engname, eng):
        seen = {}

        def do_waits(o, dl):
            waits = {}
            for d in dl:
                od = ops[d]
                if _skip_edge(od, o):
                    continue
                s, v = od.tok
                k = id(s)
                if k not in waits or waits[k][1] < v:
                    waits[k] = (s, v)
            for k, (s, v) in waits.items():
                if seen.get(k, 0) >= v:
                    continue
                eng.wait_ge(s, v)
                seen[k] = v

        for idx in order[engname]:
            o = ops[idx]
            if engname == "pe" and o.ndummy > 0:
                do_waits(o, list(o.out_deps) + list(dummy_deps))
                for _ in range(o.ndummy):
                    dummy(eng, o.bank)
            do_waits(o, o.deps)
            r = o.fn(eng)
            if o.dma:
                assert len(r) == o.ndma, (len(r), o.ndma)
                for ins in r:
                    ins.then_inc(o.tok[0], 16)
            elif o.tok is not None:
                r.then_inc(o.tok[0], 1)
        if engname == "sp":
            for s, v in finals:
                eng.wait_ge(s, v)

    with nc.Block() as block:
        block.tensor(lambda e: emit("pe", e))
        block.scalar(lambda e: emit("act", e))
        block.vector(lambda e: emit("dve", e))
        block.gpsimd(lambda e: emit("pool", e))
        block.sync(lambda e: emit("sp", e))
    return nsem[0]


def _bf(a):
    return np.asarray(a, dtype=np.float32).astype(ml_dtypes.bfloat16)


def make_consts():
    c = {}
    c["ident"] = _bf(np.eye(128))
    i = np.arange(128)[:, None]
    j = np.arange(128)[None, :]
    masks = np.full((7, 128, 256), NEG, np.float32)
    prev_ok = j >= i
    own_ok = j <= i
    masks[0, :, :128][prev_ok] = 0
    masks[0, :, 128:][own_ok] = 0
    masks[1, :, :128][prev_ok & (j >= 112)] = 0
    masks[1, :, 128:][own_ok] = 0
    masks[2, :, 128:][own_ok & (j >= 112)] = 0
    r = np.arange(128)
    jj = r // 32
    ii = r % 8
    for T in range(4):
        m = masks[3 + T]
        m[:, :128][np.arange(128)[None, :] >= ii[:, None]] = 0
        ks = np.arange(128) // 8
        ki = np.arange(128) % 8
        ok = (ks[None, :] == (4 * T + jj)[:, None]) & (ki[None, :] <= ii[:, None])
        m[:, 128:][ok] = 0
    c["masks"] = _bf(masks.transpose(1, 0, 2))
    B = np.zeros((28, 128, 128), np.float32)
    tp = np.arange(128)[:, None]
    t = np.arange(128)[None, :]
    for g, w in enumerate(WINS):
        own = ((t - tp >= 0) & (t - tp < w)).astype(np.float32) / w - (tp == t)
        B[0 + g] = own
        prv = ((t + 128 - tp > 0) & (t + 128 - tp < w)).astype(np.float32) / w
        B[4 + g] = prv
        cnt = np.minimum(np.maximum(t - 112 + 1, 1), w).astype(np.float32)
        mo = ((t - tp >= 0) & (t - tp < w) & (tp >= 112)).astype(np.float32) / cnt - ((tp == t) & (tp >= 112))
        mo_hi = mo.astype(ml_dtypes.bfloat16).astype(np.float32)
        B[8 + g] = mo_hi
        B[24 + g] = mo - mo_hi
        sp_, ip_ = tp // 8, tp % 8
        s_, i_ = t // 8, t % 8
        so = ((sp_ == s_) & (i_ - ip_ >= 0) & (i_ - ip_ < w)).astype(np.float32) / w - (tp == t)
        B[12 + g] = so
        for hh in range(2):
            hb = np.zeros((128, 128), np.float32)
            rows = np.arange(120)
            sr, rr = rows // 15, rows % 15
            ok = ((sr[:, None] + 8 * hh) == s_) & (rr[:, None] >= 16 + i_ - w)
            hb[:120] = ok.astype(np.float32) / w
            B[16 + 2 * g + hh] = hb
    c["bmat"] = _bf(B.transpose(1, 0, 2))
    inv = (np.float32(500000.0) ** (-np.arange(0, 16, 2, dtype=np.float32) / np.float32(16))).astype(np.float32)
    rope = np.zeros((128, 18, 32), np.float32)
    for ti in range(18):
        if ti == 0:
            pos = np.arange(128) - 112
        elif ti <= 16:
            pos = 16 + 128 * (ti - 1) + np.arange(128)
        else:
            pos = PAST + (np.arange(128) % 8)
        ang = pos.astype(np.float32)[:, None] * inv[None, :]
        cs_, sn_ = np.cos(ang).astype(np.float32), np.sin(ang).astype(np.float32)
        rope[:, ti, 0:8] = cs_
        rope[:, ti, 8:16] = cs_
        rope[:, ti, 16:24] = -sn_
        rope[:, ti, 24:32] = sn_
    c["rope"] = rope
    return c


def build_program(MTS=MTS, LIMIT=None, REORDER=True, DUMMY=True):
    nc = bass.Bass("TRN2", target_bir_lowering=False)
    dt_in = lambda n, s, d=F32: nc.dram_tensor(n, list(s), d, kind="ExternalInput").ap()
    dt_out = lambda n, s: nc.dram_tensor(n, list(s), F32, kind="ExternalOutput").ap()
    x_p = dt_in("x_p", [2, SEQ, D])
    x_s = dt_in("x_s", [128, D])
    meta = dt_in("meta", [16, D])
    ck = dt_in("ck", [L, 16, 128, 128])
    cv = dt_in("cv", [L, 16, 128, 128])
    stp = dt_in("stp", [L, 16, 15, 512])
    w_in = dt_in("w_in", [L, D, 1280])
    w_out = dt_in("w_out", [L, D, D])
    pool_w = dt_in("pool_w", [L, 4, 128, 128])
    w_up = dt_in("w_up", [L, D, 4096])
    w_dn = dt_in("w_dn", [L, 4096, D])
    gbc = dt_in("gbc", [L, 2, 128, D])
    pscale = dt_in("pscale", [128, L * 4])
    sinkb = dt_in("sinkb", [128, L * 8])
    sinkc = dt_in("sinkc", [128, L * 2])
    lnf = dt_in("lnf", [128, D])
    ident_d = dt_in("ident", [128, 128], BF16)
    masks_d = dt_in("masks", [128, 7, 256], BF16)
    bmat_d = dt_in("bmat", [128, 28, 128], BF16)
    rope_d = dt_in("rope", [128, 18, 32])
    y_p = dt_out("y_p", [2, SEQ, D])
    y_s = dt_out("y_s", [128, D])
    nkp = dt_out("nkp", [L, 2, 128, 128])
    nvp = dt_out("nvp", [L, 2, 128, 128])
    npp = dt_out("npp", [L, 2, 15, 512])
    nks = dt_out("nks", [L, 16, 128, 128])
    nvs = dt_out("nvs", [L, 16, 128, 128])
    nps = dt_out("nps", [L, 16, 15, 512])

    S = Sched()
    stack = contextlib.ExitStack()
    allbufs = []

    def sb(name, shape, dt=F32):
        t = stack.enter_context(nc.sbuf_tensor("s_" + name, list(shape), dt))
        b = Buf(name)
        allbufs.append(b)
        return t, b

    def sbn(name, n, shape, dt=F32):
        ts, bs = [], []
        for i in range(n):
            t, b = sb("%s%d" % (name, i), shape, dt)
            ts.append(t)
            bs.append(b)
        return ts, bs

    h_t, h_b = sbn("h", 7, [128, D])
    xn2T_t, xn2T_b = sb("xn2T", [128, 8, 7 * 128], BF16)
    xn2T_bt = [Buf("xn2T_%d" % i) for i in range(7)]
    wup_t, wup_b = sbn("wup", 2, [128, 8, 512], BF16)
    wdn_t, wdn_b = sbn("wdn", 2, [128, 4, D], BF16)
    aT_t, aT_b = sbn("aT", 2, [128, 4, 512], BF16)
    r_t, r_b = sbn("relu", 2, [128, 512])
    win_t, win_b = sb("win", [128, 8, 1280], BF16)
    wout_t, wout_b = sb("wout", [128, 8, D], BF16)
    pw_t, pw_b = sb("pw", [128, 4, 128], BF16)
    ident_t, ident_b = sb("ident", [128, 128], BF16)
    masks_t, masks_b = sb("masks", [128, 7, 256], BF16)
    bmat_t, bmat_b = sb("bmat", [128, 28, 128], BF16)
    rope_t, rope_b = sb("rope", [128, 18, 32])
    lnf_t, lnf_b = sb("lnf", [128, D])
    g_t, g_b = sbn("gbc", 2, [128, D])
    pscale_t, pscale_b = sb("pscale", [128, L * 4])
    sinkb_t, sinkb_b = sb("sinkb", [128, L * 8])
    sinkc_t, sinkc_b = sb("sinkc", [128, L * 2])
    eps_t, eps_b = sb("eps", [128, 1])
    xhat_t, xhat_b = sbn("xhat", 2, [128, D], BF16)
    xnT_t, xnT_b = sbn("xnT", 2, [128, 8, 128], BF16)
    qk_t, qk_b = sbn("qk", 2, [128, 640], BF16)
    rt_t, rt_b = sbn("ropet", 2, [128, 10, 16])
    vu_t, vu_b, kT_t, kT_b = [], [], [], []
    for l in range(L):
        a, b = sbn("vu%d_" % l, 4, [128, 640], BF16)
        vu_t.append(a); vu_b.append(b)
        a, b = sbn("kT%d_" % l, 4, [64, 2, 128], BF16)
        kT_t.append(a); kT_b.append(b)
    qT_t, qT_b = sbn("qT", 2, [64, 8, 128], BF16)
    P_t, P_b = sbn("P", 2, [128, 8, 256], BF16)
    PT_t, PT_b = sbn("PT", 2, [128, 8, 2, 128], BF16)
    a_t, a_b = sbn("a", 2, [128, 512], BF16)
    mix_t, mix_b = sbn("mixT", 2, [128, 8, 128], BF16)
    d_t, d_b = sbn("dbf", 2, [128, 4, 128], BF16)
    st_t, st_b = sbn("stat", 8, [128, 64])
    st_b2 = [Buf("statb%d" % i) for i in range(8)]
    st_b3 = [Buf("statc%d" % i) for i in range(8)]
    st_g = [[[Buf("statg%d_%d_%d" % (i, g, k)) for k in range(3)] for g in range(2)] for i in range(8)]
    P_bg = [[Buf("Pg%d_%d" % (i, g)) for g in range(2)] for i in range(2)]
    PT_bg = [[Buf("PTg%d_%d" % (i, g)) for g in range(2)] for i in range(2)]
    a_bg = [[Buf("ag%d_%d" % (i, g)) for g in range(2)] for i in range(2)]
    kc_t, kc_b = sbn("kc", 1, [128, 4, 128], BF16)
    vc_t, vc_b = sbn("vc", 1, [128, 4, 128], BF16)
    uf_t, uf_b = aT_t[1][:, 0:2, :].rearrange("p a c -> p (a c)").bitcast(F32), aT_b[1]
    kvf_t, kvf_b = aT_t[1][:, 2, :].bitcast(F32), aT_b[1]
    kcT_t, kcT_b = r_t[0][0:64, :].bitcast(BF16).rearrange("p (j k t) -> p j k t", j=4, k=2), r_b[0]
    hist_t, hist_b = r_t[1][:, :].bitcast(BF16).rearrange("p (h f) -> p h f", h=2), r_b[1]
    ps_t, ps_b = [], []
    for i in range(8):
        if i == 0:
            ps_all = stack.enter_context(nc.psum_tensor("ps_all", [128, 8, 512], F32))
        ps_t.append(ps_all[:, i, :])
        ps_b.append(Buf("ps%d" % i))
        ps_b[-1].excl = True
        ps_b[-1].bank = i
    out_b = Buf("dram_out")
    allbufs.append(out_b)

    def _nbytes(ap):
        n = 1
        for x in ap.shape:
            n *= x
        return n * (2 if ap.dtype == BF16 else 4)

    def dma(eng, out, in_, reads=(), writes=(), sembuf=None, pwrites=()):
        def fn(e):
            return [e.dma_start(out=out, in_=in_)]
        S.op(eng, fn, reads=reads, writes=writes, pwrites=pwrites, dma=1, sembuf=sembuf,
             cost=(0.1 if eng == "sp" else 1.0), nbytes=max(_nbytes(out), _nbytes(in_)))

    def dmas(eng, pairs, reads=(), writes=(), sembuf=None):
        def fn(e):
            return [e.dma_start(out=o, in_=i) for (o, i) in pairs]
        S.op(eng, fn, reads=reads, writes=writes, dma=len(pairs), sembuf=sembuf,
             cost=0.1 * len(pairs), nbytes=sum(_nbytes(o) for (o, i) in pairs))

    def act(fn, reads, writes, pw=(), cost=None):
        S.op("act", fn, reads=reads, writes=writes, pwrites=pw, cost=cost)

    def dve(fn, reads, writes, pw=(), cost=None):
        S.op("dve", fn, reads=reads, writes=writes, pwrites=pw, cost=cost)

    def pe(fn, reads, writes, pw=(), cost=None):
        S.op("pe", fn, reads=reads, writes=writes, pwrites=pw, cost=cost)

    def bcast_mid(ap2d, n):
        p, x = ap2d.shape
        return ap2d.unsqueeze(1).to_broadcast([p, n, x])

    def bcast_last(ap2d, n):
        p, x = ap2d.shape
        return ap2d.unsqueeze(2).to_broadcast([p, x, n])

    dma("sp", ident_t[:], ident_d, writes=[ident_b], sembuf=ident_b)
    dma("sp", masks_t[:], masks_d, writes=[masks_b], sembuf=masks_b)
    dma("sp", bmat_t[:], bmat_d, writes=[bmat_b], sembuf=bmat_b)
    dma("sp", rope_t[:], rope_d, writes=[rope_b], sembuf=rope_b)
    dma("sp", lnf_t[:], lnf, writes=[lnf_b], sembuf=lnf_b)
    dma("sp", pscale_t[:], pscale, writes=[pscale_b], sembuf=pscale_b)
    dma("sp", sinkb_t[:], sinkb, writes=[sinkb_b], sembuf=sinkb_b)
    dma("sp", sinkc_t[:], sinkc, writes=[sinkc_b], sembuf=sinkc_b)
    dve(lambda e: e.memset(eps_t[:], EPS), [], [eps_b])
    for l in range(L):
        dmas("sp", [(nks[l, :, 0:120, :], ck[l, :, 8:128, :]), (nvs[l, :, 0:120, :], cv[l, :, 8:128, :]),
                    (nps[l, :, 0:7, :], stp[l, :, 8:15, :])], writes=[], sembuf=out_b)

    def load_att_weights(l):
        dma("pool", win_t[:], w_in[l].rearrange("(k p) n -> p k n", p=128), writes=[win_b], sembuf=win_b)
        dma("pool", wout_t[:], w_out[l].rearrange("(k p) n -> p k n", p=128), writes=[wout_b], sembuf=wout_b)
        dma("pool", pw_t[:], pool_w[l].rearrange("g c e -> c g e"), writes=[pw_b], sembuf=pw_b)

    def load_mlp_piece(l, p):
        s = p % 2
        dma("pool", wup_t[s][:], w_up[l][:, p * 512:(p + 1) * 512].rearrange("(k p) n -> p k n", p=128),
            writes=[wup_b[s]], sembuf=wup_b[s])
        dma("pool", wdn_t[s][:], w_dn[l][p * 512:(p + 1) * 512, :].rearrange("(c p) n -> p c n", p=128),
            writes=[wdn_b[s]], sembuf=wdn_b[s])

    def tile_info(ti):
        if ti == 0:
            return dict(kind="meta", rope=0, mask=2)
        if ti == 33:
            return dict(kind="sample", rope=17)
        seq, r = divmod(ti - 1, 16)
        return dict(kind="prompt", seq=seq, r=r, rope=1 + r, mask=(1 if r == 0 else 0))

    def ring_slot(ti):
        if ti == 0:
            return 3
        return ti % 3

    def prev_slot(ti):
        info = tile_info(ti)
        if info["kind"] == "prompt":
            return 3 if info["r"] == 0 else ring_slot(ti - 1)
        return None

    cnt = {"stat0": 0, "stat1": 0, "i": 0}

    def stat(par):
        k = "stat%d" % par
        i = 4 * par + cnt[k] % 4
        cnt[k] += 1
        return st_t[i], st_b[i], st_b2[i], st_b3[i]

    def stat_g(par):
        k = "stat%d" % par
        i = 4 * par + cnt[k] % 4
        cnt[k] += 1
        return st_t[i], st_g[i]


    def norm_T(hs, gi, dstT, dstT_b, par, bank, xh=None, xhb=None):
        sT, sB, _, _ = stat(par)
        if xh is None:
            xh, xhb = xhat_t[par], xhat_b[par]
        act(lambda e: e.activation(out=xh[:], in_=h_t[hs][:], func=AF.Square, accum_out=sT[:, 0:1]),
            [h_b[hs]], [xhb, sB], cost=1.15)
        act(lambda e: e.activation(out=sT[:, 1:2], in_=sT[:, 0:1], func=AF.Ln, bias=eps_t[:, 0:1], scale=1.0 / D),
            [sB, eps_b], [sB], cost=0.3)
        act(lambda e: e.activation(out=sT[:, 2:3], in_=sT[:, 1:2], func=AF.Exp, scale=-0.5), [sB], [sB], cost=0.25)
        dve(lambda e: e.scalar_tensor_tensor(out=xh[:], in0=h_t[hs][:], scalar=sT[:, 2:3], in1=g_t[gi][:], op0=ALU.mult, op1=ALU.mult),
            [h_b[hs], sB, g_b[gi]], [xhb], cost=1.2)
        trps = ps_t[bank].bitcast(BF16)

        def tr(e):
            r = None
            for k in range(8):
                r = e.transpose(out=trps[:, k * 128:(k + 1) * 128], in_=xh[:, k * 128:(k + 1) * 128], identity=ident_t[:])
            return r
        pe(tr, [xhb, ident_b], [ps_b[bank]], cost=0.5)
        act(lambda e: e.activation(out=dstT[:, 0:4, :], in_=trps[:, 0:512].rearrange("p (k t) -> p k t", k=4), func=AF.Copy),
            [ps_b[bank]], [], pw=[dstT_b], cost=0.5)
        dve(lambda e: e.tensor_copy(out=dstT[:, 4:8, :], in_=trps[:, 512:1024].rearrange("p (k t) -> p k t", k=4)),
            [ps_b[bank]], [], pw=[dstT_b], cost=0.4)

    def att_tile(l, ti, hs, par):
        info = tile_info(ti)
        kind = info["kind"]
        B0 = 4 * par
        rs_ = ring_slot(ti)
        vu_own, vu_own_b = vu_t[l][rs_], vu_b[l][rs_]
        kT_own, kT_own_b = kT_t[l][rs_], kT_b[l][rs_]
        xnT, xnTb = xnT_t[par], xnT_b[par]
        norm_T(hs, 0, xnT, xnTb, par, B0 + 3)
        yield
        bq, bk_, bu = B0 + 1, B0 + 2, B0 + 3

        def inproj(e):
            r = None
            for k in range(8):
                for bank, (c0, c1) in ((bq, (0, 512)), (bk_, (512, 1024)), (bu, (1024, 1280))):
                    r = e.matmul(ps_t[bank][:, 0:c1 - c0], lhsT=xnT[:, k, :], rhs=win_t[:, k, c0:c1],
                                 start=(k == 0), stop=(k == 7))
            return r
        pe(inproj, [xnTb, win_b], [ps_b[bq], ps_b[bk_], ps_b[bu]], cost=3.9)
        qk, qkb = qk_t[par], qk_b[par]
        act(lambda e: e.activation(out=qk[:, 0:512], in_=ps_t[bq][:, 0:512], func=AF.Copy), [ps_b[bq]], [], pw=[qkb])
        act(lambda e: e.activation(out=qk[:, 512:640], in_=ps_t[bk_][:, 0:128], func=AF.Copy), [ps_b[bk_]], [], pw=[qkb])
        act(lambda e: e.activation(out=vu_own[:, 0:384], in_=ps_t[bk_][:, 128:512], func=AF.Copy), [ps_b[bk_]], [], pw=[vu_own_b])
        act(lambda e: e.activation(out=vu_own[:, 384:640], in_=ps_t[bu][:, 0:256], func=AF.Copy), [ps_b[bu]], [], pw=[vu_own_b])
        ri = info["rope"]
        cc = bcast_mid(rope_t[:, ri, 0:16], 10)
        ns = bcast_mid(rope_t[:, ri, 16:24], 10)
        psn = bcast_mid(rope_t[:, ri, 24:32], 10)
        x3 = ps_all[:, bq:bq + 2, :].rearrange("p a c -> p (a c)")[:, 0:640].rearrange("p (h d) -> p h d", h=10)
        qkv3 = qk[:].rearrange("p (h d) -> p h d", h=10)
        tA, tB = rt_t
        dve(lambda e: e.tensor_tensor(out=tA[:], in0=x3[:, :, 0:16], in1=cc, op=ALU.mult), [ps_b[bq], ps_b[bk_], rope_b], [rt_b[0]])
        dve(lambda e: e.tensor_tensor(out=tB[:, :, 0:8], in0=x3[:, :, 8:16], in1=ns, op=ALU.mult), [ps_b[bq], ps_b[bk_], rope_b], [], pw=[rt_b[1]])
        dve(lambda e: e.tensor_tensor(out=tB[:, :, 8:16], in0=x3[:, :, 0:8], in1=psn, op=ALU.mult), [ps_b[bq], ps_b[bk_], rope_b], [], pw=[rt_b[1]])
        dve(lambda e: e.tensor_tensor(out=qkv3[:, :, 0:16], in0=tA[:], in1=tB[:], op=ALU.add), [rt_b[0], rt_b[1]], [qkb])
        is_out = (kind == "sample") or (kind == "prompt" and info["r"] == 15)
        if is_out:
            act(lambda e: e.activation(out=kvf_t[:], in_=ps_t[bk_][:, 0:256], func=AF.Copy), [ps_b[bk_]], [kvf_b])
            act(lambda e: e.activation(out=uf_t[:, 0:256], in_=ps_t[bk_][:, 256:512], func=AF.Copy), [ps_b[bk_]], [], pw=[uf_b])
            act(lambda e: e.activation(out=uf_t[:, 256:512], in_=ps_t[bu][:, 0:256], func=AF.Copy), [ps_b[bu]], [], pw=[uf_b])
            kf3 = kvf_t[:, 0:128].rearrange("p (h d) -> p h d", h=2)
            dve(lambda e: e.tensor_tensor(out=kf3[:, :, 0:16], in0=tA[:, 8:10, :], in1=tB[:, 8:10, :], op=ALU.add),
                [rt_b[0], rt_b[1]], [kvf_b])
            if kind == "prompt":
                sq = info["seq"]
                dmas("sp", [(nkp[l, sq], kvf_t[:, 0:128]), (nvp[l, sq], kvf_t[:, 128:256])], reads=[kvf_b], sembuf=kvf_b)
                dma("sp", npp[l, sq], uf_t[113:128, :], reads=[uf_b], sembuf=uf_b)
            else:
                pairs = []
                for s_ in range(16):
                    pairs.append((nks[l, s_, 120:128, :], kvf_t[8 * s_:8 * s_ + 8, 0:128]))
                    pairs.append((nvs[l, s_, 120:128, :], kvf_t[8 * s_:8 * s_ + 8, 128:256]))
                dmas("sp", pairs, reads=[kvf_b], sembuf=kvf_b)
                dmas("sp", [(nps[l, s_, 7:15, :], uf_t[8 * s_:8 * s_ + 8, :]) for s_ in range(16)], reads=[uf_b], sembuf=uf_b)
        yield
        bqT, bkT, bdT, bpm = B0 + 0, B0 + 1, B0 + 2, B0 + 3
        qTps = ps_t[bqT][:].bitcast(BF16)
        kTps = ps_t[bkT][:].bitcast(BF16)

        def qktr(e):
            r = None
            for hh in range(8):
                r = e.transpose(out=qTps[0:64, hh * 128:(hh + 1) * 128], in_=qk[:, hh * 64:(hh + 1) * 64], identity=ident_t[:])
            for kv in range(2):
                r = e.transpose(out=kTps[0:64, kv * 128:(kv + 1) * 128], in_=qk[:, 512 + kv * 64:512 + (kv + 1) * 64], identity=ident_t[:])
            return r
        pe(qktr, [qkb, ident_b], [ps_b[bqT], ps_b[bkT]], cost=0.8)
        qT, qTb = qT_t[par], qT_b[par]
        if kind != "sample":
            act(lambda e: e.activation(out=qT[:].rearrange("p h t -> p (h t)"), in_=qTps[0:64, :], func=AF.Copy), [ps_b[bqT]], [qTb])
        else:
            for g in range(2):
                src = qTps[0:64, g * 512:(g + 1) * 512].rearrange("p (h s i) -> p h s i", h=4, s=16)
                dst = qT[:].rearrange("p h t -> p (h t)")[:, g * 512:(g + 1) * 512].rearrange("p (s h i) -> p h s i", s=16, h=4)
                dve(lambda e, src=src, dst=dst: e.tensor_copy(out=dst, in_=src), [ps_b[bqT]], [qTb])
        dve(lambda e: e.tensor_copy(out=kT_own[:].rearrange("p h t -> p (h t)"), in_=kTps[0:64, 0:256]), [ps_b[bkT]], [kT_own_b])
        mix, mixb = mix_t[par], mix_b[par]
        dbf, dbfb = d_t[par], d_b[par]
        if kind == "sample":
            for hh in range(2):
                dma("pool", hist_t[0:120, hh, :], stp[l, 8 * hh:8 * hh + 8].rearrange("s r f -> (s r) f"),
                    writes=[hist_b], sembuf=hist_b)

        def poolB(e):
            r = None
            for g in range(4):
                uo = vu_own[:, 128 + g * 128:128 + (g + 1) * 128]
                dst = ps_t[bdT][:, g * 128:(g + 1) * 128]
                if kind == "meta":
                    e.matmul(dst, lhsT=uo, rhs=bmat_t[:, 8 + g, :], start=True, stop=False)
                    r = e.matmul(dst, lhsT=uo, rhs=bmat_t[:, 24 + g, :], start=False, stop=True)
                elif kind == "prompt":
                    up = vu_t[l][prev_slot(ti)][:, 128 + g * 128:128 + (g + 1) * 128]
                    e.matmul(dst, lhsT=up, rhs=bmat_t[:, 4 + g, :], start=True, stop=False)
                    r = e.matmul(dst, lhsT=uo, rhs=bmat_t[:, 0 + g, :], start=False, stop=True)
                else:
                    for hh in range(2):
                        e.matmul(dst, lhsT=hist_t[0:120, hh, g * 128:(g + 1) * 128], rhs=bmat_t[0:120, 16 + 2 * g + hh, :],
                                 start=(hh == 0), stop=False)
                    r = e.matmul(dst, lhsT=uo, rhs=bmat_t[:, 12 + g, :], start=False, stop=True)
            return r
        rd = [vu_own_b, bmat_b]
        if kind == "prompt":
            rd.append(vu_b[l][prev_slot(ti)])
        if kind == "sample":
            rd.append(hist_b)
        pe(poolB, rd, [ps_b[bdT]])
        act(lambda e: e.activation(out=dbf[:].rearrange("p g t -> p (g t)"), in_=ps_t[bdT][:], func=AF.Copy), [ps_b[bdT]], [dbfb])
        yield

        def poolW(e):
            r = None
            for g in range(4):
                r = e.matmul(ps_t[bpm][:, g * 128:(g + 1) * 128], lhsT=pw_t[:, g, :], rhs=dbf[:, g, :], start=True, stop=True)
            return r
        pe(poolW, [dbfb, pw_b], [ps_b[bpm]], cost=0.3)
        dve(lambda e: e.tensor_tensor(out=mix[:, 4:8, :], in0=ps_t[bpm][:].rearrange("p (g t) -> p g t", g=4),
                                      in1=bcast_last(pscale_t[:, l * 4:(l + 1) * 4], 128), op=ALU.mult),
            [ps_b[bpm], pscale_b], [], pw=[mixb])
        yield
        if kind != "sample":
            ps_slot = prev_slot(ti)
            if ps_slot is None:
                kT_prev, kT_prev_b, vu_prev, vu_prev_b = kT_own, kT_own_b, vu_own, vu_own_b
            else:
                kT_prev, kT_prev_b = kT_t[l][ps_slot], kT_b[l][ps_slot]
                vu_prev, vu_prev_b = vu_t[l][ps_slot], vu_b[l][ps_slot]
            P, PT = P_t[par], PT_t[par]
            Pg, PTg, ag = P_bg[par], PT_bg[par], a_bg[par]
            asb = a_t[par]
            sT, sG = stat_g(par)
            sbanks = ((B0 + 0, B0 + 1), (B0 + 2, B0 + 3))
            mask2d = masks_t[:, info["mask"], :]

            def st_scores(g):
                ba, bb_ = sbanks[g]
                Ba = sG[g][0]

                def scores(e):
                    r = None
                    for h4 in range(4):
                        bank = ps_t[ba] if h4 < 2 else ps_t[bb_]
                        o = (h4 % 2) * 256
                        e.matmul(bank[:, o:o + 128], lhsT=qT[:, g * 4 + h4, :], rhs=kT_prev[:, g, :], start=True, stop=False)
                        e.matmul(bank[:, o + 128:o + 256], lhsT=qT[:, g * 4 + h4, :], rhs=kT_own[:, g, :], start=False, stop=False)
                        r = e.matmul(bank[:, o:o + 256], lhsT=ident_t[:], rhs=mask2d, start=False, stop=True)
                    return r
                pe(scores, [qTb, kT_prev_b, kT_own_b, ident_b, masks_b], [ps_b[ba], ps_b[bb_]], cost=1.8)
                for hf, bk in ((0, ba), (1, bb_)):
                    v3 = ps_t[bk][:].rearrange("p (h s) -> p h s", h=2)
                    cc0 = g * 4 + hf * 2
                    dve(lambda e, v3=v3, cc0=cc0: e.tensor_reduce(out=sT[:, cc0:cc0 + 2], in_=v3, axis=AX.X, op=ALU.max),
                        [ps_b[bk]], [], pw=[Ba])
                c0, c1 = g * 4, g * 4 + 4
                dve(lambda e: e.scalar_tensor_tensor(out=sT[:, 8 + c0:8 + c1], in0=sT[:, c0:c1], scalar=SCALE, in1=sinkb_t[:, l * 8 + c0:l * 8 + c1],
                                                     op0=ALU.mult, op1=ALU.max),
                    [Ba, sinkb_b], [Ba], cost=0.2)
                dve(lambda e: e.tensor_scalar(out=sT[:, 16 + c0:16 + c1], in0=sT[:, 8 + c0:8 + c1], scalar1=-1.0, scalar2=None, op0=ALU.mult),
                    [Ba], [Ba], cost=0.2)

            def st_exp(g):
                ba, bb_ = sbanks[g]
                Ba, Bb, Bc = sG[g]
                c0, c1 = g * 4, g * 4 + 4
                for h4 in range(4):
                    bk = ba if h4 < 2 else bb_
                    o = (h4 % 2) * 256
                    hh = g * 4 + h4
                    act(lambda e, bk=bk, o=o, hh=hh: e.activation(out=P[:, hh, :], in_=ps_t[bk][:, o:o + 256], func=AF.Exp,
                                                                   bias=sT[:, 16 + hh:17 + hh], scale=SCALE,
                                                                   accum_out=sT[:, 24 + hh:25 + hh]),
                        [ps_b[bk], Ba], [], pw=[Pg[g], Bb], cost=0.47)
                dve(lambda e: e.tensor_tensor(out=sT[:, 32 + c0:32 + c1], in0=sinkb_t[:, l * 8 + c0:l * 8 + c1], in1=sT[:, 16 + c0:16 + c1], op=ALU.add),
                    [Ba, sinkb_b], [Bc], cost=0.15)
                act(lambda e: e.activation(out=sT[:, 40 + c0:40 + c1], in_=sT[:, 32 + c0:32 + c1], func=AF.Exp), [Bc], [Bc], cost=0.25)
                dve(lambda e: e.tensor_tensor(out=sT[:, 48 + c0:48 + c1], in0=sT[:, 40 + c0:40 + c1], in1=sT[:, 24 + c0:24 + c1], op=ALU.add), [Bc, Bb], [Bc], cost=0.15)
                dve(lambda e: e.reciprocal(out=sT[:, 56 + c0:56 + c1], in_=sT[:, 48 + c0:48 + c1]), [Bc], [Bc], cost=0.2)

            def st_pt(g):
                bpt = sbanks[g][0]
                ptps = ps_t[bpt].bitcast(BF16)

                def ptr(e):
                    r = None
                    for h4 in range(4):
                        for hf in range(2):
                            c = (h4 * 2 + hf) * 128
                            r = e.transpose(out=ptps[:, c:c + 128], in_=P[:, g * 4 + h4, hf * 128:(hf + 1) * 128], identity=ident_t[:])
                    return r
                pe(ptr, [Pg[g], ident_b], [ps_b[bpt]], cost=0.45)
                dst = PT[:, g * 4:(g + 1) * 4, :, :].rearrange("p h f t -> p (h f t)")
                if g == 0:
                    act(lambda e: e.activation(out=dst, in_=ptps[:, :], func=AF.Copy), [ps_b[bpt]], [PTg[g]])
                else:
                    dve(lambda e: e.tensor_copy(out=dst, in_=ptps[:, :]), [ps_b[bpt]], [PTg[g]])

            def st_pv(g):
                bpv = sbanks[g][1]
                Bc = sG[g][2]

                def pv(e):
                    r = None
                    for h4 in range(4):
                        hh = g * 4 + h4
                        dst = ps_t[bpv][:, h4 * 64:(h4 + 1) * 64]
                        e.matmul(dst, lhsT=PT[:, hh, 0, :], rhs=vu_prev[:, g * 64:(g + 1) * 64], start=True, stop=False)
                        r = e.matmul(dst, lhsT=PT[:, hh, 1, :], rhs=vu_own[:, g * 64:(g + 1) * 64], start=False, stop=True)
                    return r
                pe(pv, [PTg[g], vu_prev_b, vu_own_b], [ps_b[bpv]], cost=0.4)
                dve(lambda e: e.tensor_tensor(out=asb[:, g * 256:(g + 1) * 256].rearrange("p (h d) -> p h d", h=4),
                                              in0=ps_t[bpv][:, 0:256].rearrange("p (h d) -> p h d", h=4),
                                              in1=bcast_last(sT[:, 56 + g * 4:60 + g * 4], 64), op=ALU.mult), [ps_b[bpv], Bc], [ag[g]])

            def st_at(g):
                baT = sbanks[g][0]
                aTps = ps_t[baT].bitcast(BF16)

                def atr(e):
                    r = None
                    for c in range(2):
                        cc_ = g * 2 + c
                        r = e.transpose(out=aTps[:, c * 128:(c + 1) * 128], in_=asb[:, cc_ * 128:(cc_ + 1) * 128], identity=ident_t[:])
                    return r
                pe(atr, [ag[g], ident_b], [ps_b[baT]], cost=0.15)
                act(lambda e: e.activation(out=mix[:, g * 2:g * 2 + 2, :].rearrange("p c t -> p (c t)"), in_=aTps[:, 0:256], func=AF.Copy),
                    [ps_b[baT]], [], pw=[mixb])

            st_scores(0)
            yield
            st_scores(1)
            st_exp(0)
            yield
            st_pt(0)
            st_exp(1)
            yield
            st_pv(0)
            st_pt(1)
            yield
            st_at(0)
            st_pv(1)
            yield
            st_at(1)
            yield
        else:
            for T in range(4):
                kp = 0
                dma("pool", kc_t[kp][:], ck[l, 4 * T:4 * T + 4].rearrange("s k f -> k s f"), writes=[kc_b[kp]], sembuf=kc_b[kp])
                dma("pool", vc_t[kp][:], cv[l, 4 * T:4 * T + 4].rearrange("s k f -> k s f"), writes=[vc_b[kp]], sembuf=vc_b[kp])
                bkc = B0 + 0
                kcTps = ps_t[bkc][:].bitcast(BF16)

                def kctr(e, kp=kp, kcTps=kcTps):
                    r = None
                    for j in range(4):
                        for kv in range(2):
                            c = (j * 2 + kv) * 128
                            r = e.transpose(out=kcTps[0:64, c:c + 128], in_=kc_t[kp][:, j, kv * 64:(kv + 1) * 64], identity=ident_t[:])
                    return r
                pe(kctr, [kc_b[kp], ident_b], [ps_b[bkc]])
                act(lambda e, kcTps=kcTps: e.activation(out=kcT_t.rearrange("p j k t -> p (j k t)"), in_=kcTps[0:64, :], func=AF.Copy),
                    [ps_b[bkc]], [kcT_b])
                yield
                for g in range(2):
                    sT, sB, sB2, sB3 = stat(par)
                    pp = g
                    P, Pb = P_t[par], P_bg[par][0]
                    PT, PTb = PT_t[par], PT_bg[par][0]
                    sbk = B0 + 1 + g

                    def sscores(e, T=T, g=g, sbk=sbk):
                        r = None
                        for j in range(4):
                            s = 4 * T + j
                            lt = qT[:].rearrange("p h t -> p (h t)")[:, (g * 16 + s) * 32:(g * 16 + s + 1) * 32]
                            e.matmul(ps_t[sbk][32 * j:32 * j + 32, 0:128], lhsT=lt, rhs=kcT_t[:, j, g, :], start=True, stop=True,
                                     tile_position=(0, 32 * j))
                            r = e.matmul(ps_t[sbk][32 * j:32 * j + 32, 128:256], lhsT=lt, rhs=kT_own[:, g, :], start=True, stop=True,
                                         tile_position=(0, 32 * j))
                        return r
                    pe(sscores, [qTb, kcT_b, kT_own_b], [ps_b[sbk]])
                    sv = ps_t[sbk][:, 0:256]
                    dve(lambda e, sv=sv, T=T: e.tensor_tensor(out=sv, in0=sv, in1=masks_t[:, 3 + T, :], op=ALU.add),
                        [ps_b[sbk], masks_b], [ps_b[sbk]])
                    dve(lambda e, sv=sv, sT=sT: e.tensor_reduce(out=sT[:, 0:1], in_=sv, axis=AX.X, op=ALU.max), [ps_b[sbk]], [sB])
                    sc = sinkc_t[:, l * 2 + g:l * 2 + g + 1]
                    dve(lambda e, sT=sT, sc=sc: e.scalar_tensor_tensor(out=sT[:, 1:2], in0=sT[:, 0:1], scalar=SCALE, in1=sc,
                                                                       op0=ALU.mult, op1=ALU.max), [sB, sinkc_b], [sB])
                    dve(lambda e, sT=sT: e.tensor_scalar(out=sT[:, 2:3], in0=sT[:, 1:2], scalar1=-1.0, scalar2=None, op0=ALU.mult), [sB], [sB])
                    yield
                    act(lambda e, sv=sv, sT=sT, P=P, g=g: e.activation(out=P[:, g, :], in_=sv, func=AF.Exp, bias=sT[:, 2:3], scale=SCALE,
                                                                       accum_out=sT[:, 3:4]), [ps_b[sbk], sB], [sB2], pw=[Pb])
                    dve(lambda e, sT=sT, sc=sc: e.tensor_tensor(out=sT[:, 4:5], in0=sc, in1=sT[:, 2:3], op=ALU.add), [sB, sinkc_b], [sB3])
                    act(lambda e, sT=sT: e.activation(out=sT[:, 5:6], in_=sT[:, 4:5], func=AF.Exp), [sB3], [sB3])
                    dve(lambda e, sT=sT: e.tensor_tensor(out=sT[:, 6:7], in0=sT[:, 5:6], in1=sT[:, 3:4], op=ALU.add), [sB3, sB2], [sB3])
                    dve(lambda e, sT=sT: e.reciprocal(out=sT[:, 7:8], in_=sT[:, 6:7]), [sB3], [sB3])
                    yield
                    bpt = B0 + 3
                    ptps = ps_t[bpt][:].bitcast(BF16)

                    def sptr(e, P=P, ptps=ptps, g=g):
                        e.transpose(out=ptps[:, 0:128], in_=P[:, g, 0:128], identity=ident_t[:])
                        return e.transpose(out=ptps[:, 128:256], in_=P[:, g, 128:256], identity=ident_t[:])
                    pe(sptr, [Pb, ident_b], [ps_b[bpt]])
                    act(lambda e, PT=PT, ptps=ptps, g=g: e.activation(out=PT[:, g, :, :].rearrange("p f t -> p (f t)"), in_=ptps[:, 0:256], func=AF.Copy),
                        [ps_b[bpt]], [], pw=[PTb])
                    yield
                    bsv = B0 + 0

                    def spv(e, PT=PT, g=g, kp=kp, bsv=bsv):
                        r = None
                        for j in range(4):
                            dst = ps_t[bsv][32 * j:32 * j + 32, 0:64]
                            e.matmul(dst, lhsT=PT[:, g, 0, 32 * j:32 * j + 32], rhs=vc_t[kp][:, j, g * 64:(g + 1) * 64], start=True, stop=False,
                                     tile_position=(0, 32 * j))
                            r = e.matmul(dst, lhsT=PT[:, g, 1, 32 * j:32 * j + 32], rhs=vu_own[:, g * 64:(g + 1) * 64], start=False, stop=True,
                                         tile_position=(0, 32 * j))
                        return r
                    pe(spv, [PTb, vc_b[kp], vu_own_b], [ps_b[bsv]])
                    asb, asbb = a_t[par], a_bg[par][0]
                    dve(lambda e, asb=asb, sT=sT, g=g, bsv=bsv: e.tensor_scalar(out=asb[:, g * 64:(g + 1) * 64], in0=ps_t[bsv][:, 0:64], scalar1=sT[:, 7:8], scalar2=None, op0=ALU.mult),
                        [ps_b[bsv], sB3], [], pw=[asbb])
                    yield
                    aTs = ps_t[bpt][:].bitcast(BF16)

                    def satr(e, asb=asb, aTs=aTs, g=g):
                        e.transpose(out=aTs[0:64, 0:128], in_=asb[:, g * 64:(g + 1) * 64], identity=ident_t[:])
                        return e.transpose(out=aTs[64:128, 0:128], in_=asb[:, g * 64:(g + 1) * 64], identity=ident_t[:], tile_position=(0, 64))
                    pe(satr, [asbb, ident_b], [ps_b[bpt]])
                    for h4 in range(4):
                        p0 = 0 if h4 % 2 == 0 else 64
                        c = g * 2 + h4 // 2
                        src = aTs[p0:p0 + 64, 0:128].rearrange("p (j h i) -> p j h i", j=4, h=4)[:, :, h4, :]
                        dst = mix[p0:p0 + 64, c, 32 * T:32 * T + 32].rearrange("p (j i) -> p j i", j=4)
                        if h4 < 2:
                            act(lambda e, src=src, dst=dst: e.activation(out=dst, in_=src, func=AF.Copy), [ps_b[bpt]], [], pw=[mixb])
                        else:
                            dve(lambda e, src=src, dst=dst: e.tensor_copy(out=dst, in_=src), [ps_b[bpt]], [], pw=[mixb])
                    yield
        bo = B0

        def oproj(e):
            r = None
            for c in range(8):
                for hf in range(2):
                    r = e.matmul(ps_t[bo + hf][:], lhsT=mix[:, c, :], rhs=wout_t[:, c, hf * 512:(hf + 1) * 512], start=(c == 0), stop=(c == 7))
            return r
        pe(oproj, [mixb, wout_b], [ps_b[bo], ps_b[bo + 1]], cost=3.6)
        for hf in range(2):
            dve(lambda e, hf=hf: e.tensor_tensor(out=h_t[hs][:, hf * 512:(hf + 1) * 512], in0=h_t[hs][:, hf * 512:(hf + 1) * 512],
                                                 in1=ps_t[bo + hf][:], op=ALU.add), [h_b[hs], ps_b[bo + hf]], [h_b[hs]])
        yield
        norm_T(hs, 1, xn2T_t[:, :, hs * 128:(hs + 1) * 128], xn2T_bt[hs], par, B0 + 2,
               xh=P_t[par][:, 0:4, :].rearrange("p h s -> p (h s)"), xhb=P_bg[par][0])

    def att_phase(l, tiles):
        LAG = 3
        gens = [att_tile(l, ti, hs, hs % 2) for hs, ti in enumerate(tiles)]
        inflight, progress, nxt = [], {}, 0
        while nxt < len(gens) or inflight:
            if nxt < len(gens) and len(inflight) < 2 and (not inflight or progress[inflight[-1]] >= LAG):
                inflight.append(nxt)
                progress[nxt] = 0
                nxt += 1
            for g in list(inflight):
                try:
                    next(gens[g])
                    progress[g] += 1
                except StopIteration:
                    inflight.remove(g)

    def mlp_phase(l, ntiles):
        nsub = (ntiles + 3) // 4
        base, rem = divmod(ntiles, nsub)
        subs, a0 = [], 0
        for i_ in range(nsub):
            a1 = a0 + base + (1 if i_ < rem else 0)
            subs.append((a0, a1))
            a0 = a1
        upc = [0]
        for p in range(NPIECE):
            s = p % 2
            for (t0, t1) in subs:
                n = (t1 - t0) * 128
                ai = upc[0] % 2
                upc[0] += 1
                aT, aTb = aT_t[ai], aT_b[ai]
                pend = None
                for fc in range(4):
                    bk = fc % 4

                    def up(e, fc=fc, bk=bk, t0=t0, n=n, s=s):
                        r = None
                        for k in range(8):
                            r = e.matmul(ps_t[bk][:, 0:n], lhsT=wup_t[s][:, k, fc * 128:(fc + 1) * 128],
                                         rhs=xn2T_t[:, k, t0 * 128:t0 * 128 + n], start=(k == 0), stop=(k == 7))
                        return r
                    pe(up, [wup_b[s]] + [xn2T_bt[t_] for t_ in range(t0, t1)], [ps_b[bk]], cost=1.75 * n / 512.0)
                    ri = fc % 2
                    act(lambda e, bk=bk, n=n, ri=ri: e.activation(out=r_t[ri][:, 0:n], in_=ps_t[bk][:, 0:n], func=AF.Relu),
                        [ps_b[bk]], [r_b[ri]], cost=0.55)
                    if pend is not None:
                        pend()
                    pend = (lambda ri=ri, n=n, fc=fc, aT=aT, aTb=aTb:
                            act(lambda e: e.activation(out=aT[:, fc, 0:n], in_=r_t[ri][:, 0:n], func=AF.Square), [r_b[ri]], [], pw=[aTb], cost=0.45))
                pend()
                for t in range(t0, t1):
                    db = 4 + 2 * (t % 2)

                    def dn(e, t=t, t0=t0, db=db, aT=aT, s=s):
                        r = None
                        for fc in range(4):
                            for hf in range(2):
                                r = e.matmul(ps_t[db + hf][:], lhsT=aT[:, fc, (t - t0) * 128:(t - t0 + 1) * 128],
                                             rhs=wdn_t[s][:, fc, hf * 512:(hf + 1) * 512], start=(fc == 0), stop=(fc == 3))
                        return r
                    pe(dn, [aTb, wdn_b[s]], [ps_b[db], ps_b[db + 1]], cost=1.75)
                    for hf in range(2):
                        dve(lambda e, t=t, hf=hf, db=db: e.tensor_tensor(out=h_t[t][:, hf * 512:(hf + 1) * 512], in0=h_t[t][:, hf * 512:(hf + 1) * 512],
                                                                          in1=ps_t[db + hf][:], op=ALU.add), [h_b[t], ps_b[db + hf]], [h_b[t]])
            if p + 2 < NPIECE:
                load_mlp_piece(l, p + 2)

    def final_tile(ti, hs, last_mt=False):
        info = tile_info(ti)
        if info["kind"] == "meta":
            return
        par = hs % 2
        sT, sB, _, _ = stat(par)
        xh, xhb = xhat_t[par], xhat_b[par]
        act(lambda e: e.activation(out=xh[:], in_=h_t[hs][:], func=AF.Square, accum_out=sT[:, 0:1]), [h_b[hs]], [xhb, sB])
        act(lambda e: e.activation(out=sT[:, 1:2], in_=sT[:, 0:1], func=AF.Ln, bias=eps_t[:, 0:1], scale=1.0 / D), [sB, eps_b], [sB])
        act(lambda e: e.activation(out=sT[:, 2:3], in_=sT[:, 1:2], func=AF.Exp, scale=-0.5), [sB], [sB])
        if info["kind"] == "prompt":
            dst = y_p[info["seq"], info["r"] * 128:(info["r"] + 1) * 128, :]
        else:
            dst = y_s
        if last_mt:
            dve(lambda e: e.scalar_tensor_tensor(out=h_t[hs][:], in0=h_t[hs][:], scalar=sT[:, 2:3], in1=lnf_t[:], op0=ALU.mult, op1=ALU.mult),
                [h_b[hs], sB, lnf_b], [h_b[hs]], cost=1.2)
            dma("sp", dst, h_t[hs][:], reads=[h_b[hs]], sembuf=h_b[hs])
        else:
            ys = aT_t[par][:].rearrange("p a c -> p (a c)").bitcast(F32)
            dve(lambda e: e.scalar_tensor_tensor(out=ys, in0=h_t[hs][:], scalar=sT[:, 2:3], in1=lnf_t[:], op0=ALU.mult, op1=ALU.mult),
                [h_b[hs], sB, lnf_b], [aT_b[par]], cost=1.2)
            dma("sp", dst, ys, reads=[aT_b[par]], sembuf=aT_b[par])

    load_att_weights(0)
    for mi, tiles in enumerate(MTS):
        for hs, ti in enumerate(tiles):
            info = tile_info(ti)
            if info["kind"] == "meta":
                dve(lambda e, hs=hs: e.memset(h_t[hs][:], 0.0), [], [h_b[hs]])
                dma("sp", h_t[hs][112:128, :], meta, writes=[h_b[hs]], sembuf=h_b[hs])
            elif info["kind"] == "prompt":
                dma("sp", h_t[hs][:], x_p[info["seq"], info["r"] * 128:(info["r"] + 1) * 128, :], writes=[h_b[hs]], sembuf=h_b[hs])
            else:
                dma("sp", h_t[hs][:], x_s, writes=[h_b[hs]], sembuf=h_b[hs])
        for l in range(L):
            S.epoch += 1
            load_mlp_piece(l, 0)
            load_mlp_piece(l, 1)
            for gi in range(2):
                dma("sp", g_t[gi][:], gbc[l, gi], writes=[g_b[gi]], sembuf=g_b[gi])
            att_phase(l, tiles)
            nl, nm = (l + 1, mi) if l + 1 < L else (0, mi + 1)
            if nm < len(MTS):
                load_att_weights(nl)
            mlp_phase(l, len(tiles))
        for hs, ti in enumerate(tiles):
            final_tile(ti, hs, last_mt=(mi == len(MTS) - 1))

    if LIMIT is not None:
        S.ops = S.ops[:LIMIT]
    def dummy_mm(e, bank):
        e.matmul(ps_t[bank][:, 0:512], lhsT=ident_t[:], rhs=bmat_t[:, 0:4, :].rearrange("p a t -> p (a t)"), start=True, stop=True)
    nsem = lower(nc, S, stack, allbufs, reorder=REORDER, dummy=(dummy_mm if DUMMY else None),
                 dummy_deps=list(ident_b.writers) + list(bmat_b.writers))
    stack.close()
    return nc, nsem, len(S.ops)


_CACHE = {}


def kernel(x_prompt, x_sample, cache_k, cache_v, state_pool, meta_tokens, ln1, w_in, attn_sinks,
           pool_w, pool_scale, w_out, ln2, w_up, w_down, ln_f):
    f = lambda a: np.ascontiguousarray(np.asarray(a, dtype=np.float32))
    x_prompt, x_sample, cache_k, cache_v, state_pool = map(f, (x_prompt, x_sample, cache_k, cache_v, state_pool))
    meta_tokens, ln1, w_in, attn_sinks, pool_w, pool_scale, w_out, ln2, w_up, w_down, ln_f = map(
        f, (meta_tokens, ln1, w_in, attn_sinks, pool_w, pool_scale, w_out, ln2, w_up, w_down, ln_f))
    if "nc" not in _CACHE:
        _CACHE["nc"] = build_program()[0]
        _CACHE["consts"] = make_consts()
    nc = _CACHE["nc"]
    consts = _CACHE["consts"]
    gbc = np.broadcast_to(np.stack([ln1, ln2], 1).reshape(L, 2, 1, D), (L, 2, 128, D))
    psc = pool_scale.reshape(L, 4, 128).transpose(2, 0, 1).reshape(128, L * 4)
    skb = np.broadcast_to(attn_sinks.reshape(1, L * 8), (128, L * 8))
    rows = np.arange(128)
    h4 = (rows // 8) % 4
    skc = np.stack([attn_sinks[l, kv * 4 + h4] for l in range(L) for kv in range(2)], axis=1)
    lnfb = np.broadcast_to(ln_f.reshape(1, D), (128, D))
    shared = {
        "meta": meta_tokens, "w_in": w_in, "w_out": w_out, "pool_w": pool_w, "w_up": w_up, "w_dn": w_down,
        "gbc": f(gbc), "pscale": f(psc), "sinkb": f(skb), "sinkc": f(skc), "lnf": f(lnfb),
        "ident": consts["ident"], "masks": consts["masks"], "bmat": consts["bmat"], "rope": consts["rope"],
    }
    in_maps = []
    for c in range(NCORES):
        m = dict(shared)
        m["x_p"] = x_prompt[2 * c:2 * c + 2]
        m["x_s"] = x_sample[16 * c:16 * c + 16].reshape(128, D)
        m["ck"] = f(cache_k[:, 16 * c:16 * c + 16].reshape(L, 16, 128, 128))
        m["cv"] = f(cache_v[:, 16 * c:16 * c + 16].reshape(L, 16, 128, 128))
        m["stp"] = f(state_pool[:, 16 * c:16 * c + 16])
        in_maps.append(m)
    res = run_bass_kernel_spmd(nc, in_maps, core_ids=list(range(NCORES)))
    R = res.results
    y_prompt = np.concatenate([R[c]["y_p"] for c in range(NCORES)], 0)
    y_sample = np.concatenate([R[c]["y_s"].reshape(16, 8, D) for c in range(NCORES)], 0)
    nkp = np.concatenate([R[c]["nkp"] for c in range(NCORES)], 1).reshape(L, 16, 128, 2, 64)
    nvp = np.concatenate([R[c]["nvp"] for c in range(NCORES)], 1).reshape(L, 16, 128, 2, 64)
    npp = np.concatenate([R[c]["npp"] for c in range(NCORES)], 1)
    nks = np.concatenate([R[c]["nks"] for c in range(NCORES)], 1).reshape(L, 128, 128, 2, 64)
    nvs = np.concatenate([R[c]["nvs"] for c in range(NCORES)], 1).reshape(L, 128, 128, 2, 64)
    nps = np.concatenate([R[c]["nps"] for c in range(NCORES)], 1)
    return (y_prompt.astype(np.float32), y_sample.astype(np.float32), nkp, nvp, npp, nks, nvs, nps)
```

```python
import contextlib
import numpy as np
import ml_dtypes
import concourse.bass as bass
import concourse.mybir as mybir
from concourse.bass_utils import run_bass_kernel_spmd

F32 = mybir.dt.float32
BF16 = mybir.dt.bfloat16
AF = mybir.ActivationFunctionType
ALU = mybir.AluOpType
AX = mybir.AxisListType

D = 1024
L = 2
NCORES = 8
SEQ = 2048
PAST = 16384
NEG = -30000.0
SCALE = 0.125
EPS = 1e-5
WINS = (2, 4, 8, 16)
NT = 34
MTS = [list(range(0, 7)), list(range(7, 14)), list(range(14, 21)), list(range(21, 28)), list(range(28, 34))]
NPIECE = 8


class Buf:
    def __init__(self, name):
        self.name = name
        self.writers = []
        self.gen_base = []
        self.readers = []
        self.sem = None
        self.cnt = 0


class Op:
    __slots__ = ("eng", "fn", "deps", "dma", "ndma", "sembuf", "tok", "epoch", "has_dep", "idx", "cost", "start", "end",
                 "in_deps", "out_deps", "bank", "ndummy")


DEFAULT_COST = {"pe": 0.6, "act": 0.5, "dve": 0.5, "pool": 0.3, "sp": 0.1}


class Sched:
    def __init__(self):
        self.ops = []
        self.epoch = 0

    def op(self, eng, fn, reads=(), writes=(), pwrites=(), dma=0, sembuf=None, cost=None, nbytes=0):
        o = Op()
        o.eng, o.fn, o.dma, o.ndma, o.sembuf = eng, fn, bool(dma), dma, sembuf
        o.epoch = self.epoch
        o.has_dep = False
        o.tok = None
        o.idx = len(self.ops)
        o.cost = (cost if cost is not None else DEFAULT_COST[eng], nbytes)
        deps = set()
        in_deps = set()
        o.bank = None
        o.ndummy = 0
        for b in reads:
            if not getattr(b, "excl", False):
                in_deps.update(b.writers)
        for b in list(writes) + list(pwrites):
            if getattr(b, "excl", False) and o.bank is None:
                o.bank = b.bank
        o.in_deps = in_deps
        for b in reads:
            deps.update(b.writers)
            if getattr(b, "excl", False):
                for r_ in b.readers:
                    if self.ops[r_].eng != eng:
                        deps.add(r_)
        out_deps = set()
        for b in writes:
            out_deps.update(b.writers)
            out_deps.update(b.readers)
            out_deps.update(b.gen_base)
        for b in pwrites:
            if b.readers:
                b.gen_base = list(b.readers)
                b.writers = []
                b.readers = []
            out_deps.update(b.gen_base)
            if getattr(b, "excl", False):
                for w_ in b.writers:
                    if self.ops[w_].eng != eng:
                        out_deps.add(w_)
        out_deps.discard(o.idx)
        o.out_deps = out_deps
        deps.update(out_deps)
        deps.discard(o.idx)
        o.deps = sorted(deps)
        self.ops.append(o)
        for b in reads:
            b.readers.append(o.idx)
        for b in writes:
            b.writers = [o.idx]
            b.gen_base = [o.idx]
            b.readers = []
        for b in pwrites:
            b.writers.append(o.idx)
        return o.idx


def _skip_edge(od, o):
    return od.eng == "pe" and o.eng == "pe" and not od.dma and not o.dma


ENGS = ("pe", "act", "dve", "pool", "sp")
DMA_BW = 350e3


PRIO_MODE = "cp"
SLACK = 0.0
PERT_AMP = 0.0
PERT_SEED = 0
HOP_LAT = 0.0
SELF_LAT = 0.0


def list_schedule(ops):
    n = len(ops)
    succ = [[] for _ in range(n)]
    indeg = [0] * n
    for o in ops:
        indeg[o.idx] = len(o.deps)
        for d in o.deps:
            succ[d].append(o.idx)
    cp = [0.0] * n
    for i in range(n - 1, -1, -1):
        o = ops[i]
        c = o.cost[0] + (2.0 + o.cost[1] / DMA_BW if o.dma else 0.0)
        m = 0.0
        for s_ in succ[i]:
            if cp[s_] > m:
                m = cp[s_]
        cp[i] = c + m
    if PERT_AMP > 0.0:
        rs = np.random.RandomState(PERT_SEED)
        pert = 1.0 + PERT_AMP * (rs.rand(n) - 0.5)
        cp = [c_ * p_ for c_, p_ in zip(cp, pert)]
    ready_t = [0.0] * n
    pend = {e: set() for e in ENGS}
    for o in ops:
        if indeg[o.idx] == 0:
            pend[o.eng].add(o.idx)
    eng_free = {e: 0.0 for e in ENGS}
    dma_free = [0.0]
    order = {e: [] for e in ENGS}
    done = 0
    while done < n:
        best = None
        for e in ENGS:
            if not pend[e]:
                continue
            tfree = eng_free[e]
            cand = None
            for idx in pend[e]:
                st = max(tfree, ready_t[idx])
                if PRIO_MODE == "cp":
                    key = (max(st, tfree + SLACK), -cp[idx], idx, st)
                else:
                    key = (st, idx)
                if cand is None or key < cand:
                    cand = key
            k2 = (cand[-1], cand[2]) if PRIO_MODE == "cp" else (cand[0], cand[-1])
            if best is None or k2 < best[0]:
                best = (k2, e)
        (st, idx), e = best
        pend[e].discard(idx)
        o = ops[idx]
        c, nb = o.cost
        if o.dma:
            issue_end = st + c
            xfer_start = max(issue_end, dma_free[0])
            xfer = nb / DMA_BW
            dma_free[0] = xfer_start + xfer
            fin = xfer_start + xfer + 2.0
            eng_free[e] = issue_end
        else:
            fin = st + c
            eng_free[e] = fin
        o.start, o.end = st, fin
        order[e].append(idx)
        done += 1
        for s_ in succ[idx]:
            f2 = fin + (HOP_LAT if ops[s_].eng != e else SELF_LAT)
            if f2 > ready_t[s_]:
                ready_t[s_] = f2
            indeg[s_] -= 1
            if indeg[s_] == 0:
                pend[ops[s_].eng].add(s_)
    return order


DUMMY_MIN_GAP = 0.4
DUMMY_FILL = 0.6
DUMMY_MAX = 2


def lower(nc, sched, stack, final_bufs, reorder=True, dummy=None, dummy_deps=()):
    ops = sched.ops
    for o in ops:
        for d in o.deps:
            od = ops[d]
            if _skip_edge(od, o):
                continue
            od.has_dep = True
    for d in dummy_deps:
        ops[d].has_dep = True
    if reorder:
        order = list_schedule(ops)
    else:
        order = {e: [o.idx for o in ops if o.eng == e] for e in ENGS}
    esem = {}
    ecnt = {}
    nsem = [0]

    def newsem(name):
        nsem[0] += 1
        return stack.enter_context(nc.semaphore(name))

    for e in ENGS:
        for idx in order[e]:
            o = ops[idx]
            if o.dma:
                b = o.sembuf
                if b.sem is None:
                    b.sem = newsem("d_" + b.name)
                b.cnt += 16 * o.ndma
                o.tok = (b.sem, b.cnt)
            elif o.has_dep:
                key = (o.eng, o.epoch)
                if key not in esem:
                    esem[key] = newsem("e_%s_%d" % key)
                    ecnt[key] = 0
                ecnt[key] += 1
                o.tok = (esem[key], ecnt[key])
    finals = [(b.sem, b.cnt) for b in final_bufs if b.sem is not None]

    if dummy is not None:
        prev_end = 0.0
        for idx in order["pe"]:
            o = ops[idx]
            gap = o.start - prev_end
            prev_end = o.end
            if o.bank is not None and gap > DUMMY_MIN_GAP and len(o.in_deps - o.out_deps) > 0:
                o.ndummy = min(DUMMY_MAX, int(gap * DUMMY_FILL / 0.22))

    def emit(engname, eng):
        seen = {}

        def do_waits(o, dl):
            waits = {}
            for d in dl:
                od = ops[d]
                if _skip_edge(od, o):
                    continue
                s, v = od.tok
                k = id(s)
                if k not in waits or waits[k][1] < v:
                    waits[k] = (s, v)
            for k, (s, v) in waits.items():
                if seen.get(k, 0) >= v:
                    continue
                eng.wait_ge(s, v)
                seen[k] = v

        for idx in order[engname]:
            o = ops[idx]
            if engname == "pe" and o.ndummy > 0:
                do_waits(o, list(o.out_deps) + list(dummy_deps))
                for _ in range(o.ndummy):
                    dummy(eng, o.bank)
            do_waits(o, o.deps)
            r = o.fn(eng)
            if o.dma:
                assert len(r) == o.ndma, (len(r), o.ndma)
                for ins in r:
                    ins.then_inc(o.tok[0], 16)
            elif o.tok is not None:
                r.then_inc(o.tok[0], 1)
        if engname == "sp":
            for s, v in finals:
                eng.wait_ge(s, v)

    with nc.Block() as block:
        block.tensor(lambda e: emit("pe", e))
        block.scalar(lambda e: emit("act", e))
        block.vector(lambda e: emit("dve", e))
        block.gpsimd(lambda e: emit("pool", e))
        block.sync(lambda e: emit("sp", e))
    return nsem[0]


def _bf(a):
    return np.asarray(a, dtype=np.float32).astype(ml_dtypes.bfloat16)


def make_consts():
    c = {}
    c["ident"] = _bf(np.eye(128))
    i = np.arange(128)[:, None]
    j = np.arange(128)[None, :]
    masks = np.full((7, 128, 256), NEG, np.float32)
    prev_ok = j >= i
    own_ok = j <= i
    masks[0, :, :128][prev_ok] = 0
    masks[0, :, 128:][own_ok] = 0
    masks[1, :, :128][prev_ok & (j >= 112)] = 0
    masks[1, :, 128:][own_ok] = 0
    masks[2, :, 128:][own_ok & (j >= 112)] = 0
    r = np.arange(128)
    jj = r // 32
    ii = r % 8
    for T in range(4):
        m = masks[3 + T]
        m[:, :128][np.arange(128)[None, :] >= ii[:, None]] = 0
        ks = np.arange(128) // 8
        ki = np.arange(128) % 8
        ok = (ks[None, :] == (4 * T + jj)[:, None]) & (ki[None, :] <= ii[:, None])
        m[:, 128:][ok] = 0
    c["masks"] = _bf(masks.transpose(1, 0, 2))
    B = np.zeros((28, 128, 128), np.float32)
    tp = np.arange(128)[:, None]
    t = np.arange(128)[None, :]
    for g, w in enumerate(WINS):
        own = ((t - tp >= 0) & (t - tp < w)).astype(np.float32) / w - (tp == t)
        B[0 + g] = own
        prv = ((t + 128 - tp > 0) & (t + 128 - tp < w)).astype(np.float32) / w
        B[4 + g] = prv
        cnt = np.minimum(np.maximum(t - 112 + 1, 1), w).astype(np.float32)
        mo = ((t - tp >= 0) & (t - tp < w) & (tp >= 112)).astype(np.float32) / cnt - ((tp == t) & (tp >= 112))
        mo_hi = mo.astype(ml_dtypes.bfloat16).astype(np.float32)
        B[8 + g] = mo_hi
        B[24 + g] = mo - mo_hi
        sp_, ip_ = tp // 8, tp % 8
        s_, i_ = t // 8, t % 8
        so = ((sp_ == s_) & (i_ - ip_ >= 0) & (i_ - ip_ < w)).astype(np.float32) / w - (tp == t)
        B[12 + g] = so
        for hh in range(2):
            hb = np.zeros((128, 128), np.float32)
            rows = np.arange(120)
            sr, rr = rows // 15, rows % 15
            ok = ((sr[:, None] + 8 * hh) == s_) & (rr[:, None] >= 16 + i_ - w)
            hb[:120] = ok.astype(np.float32) / w
            B[16 + 2 * g + hh] = hb
    c["bmat"] = _bf(B.transpose(1, 0, 2))
    inv = (np.float32(500000.0) ** (-np.arange(0, 16, 2, dtype=np.float32) / np.float32(16))).astype(np.float32)
    rope = np.zeros((128, 18, 32), np.float32)
    for ti in range(18):
        if ti == 0:
            pos = np.arange(128) - 112
        elif ti <= 16:
            pos = 16 + 128 * (ti - 1) + np.arange(128)
        else:
            pos = PAST + (np.arange(128) % 8)
        ang = pos.astype(np.float32)[:, None] * inv[None, :]
        cs_, sn_ = np.cos(ang).astype(np.float32), np.sin(ang).astype(np.float32)
        rope[:, ti, 0:8] = cs_
        rope[:, ti, 8:16] = cs_
        rope[:, ti, 16:24] = -sn_
        rope[:, ti, 24:32] = sn_
    c["rope"] = rope
    return c


def build_program(MTS=MTS, LIMIT=None, REORDER=True, DUMMY=True):
    nc = bass.Bass("TRN2", target_bir_lowering=False)
    dt_in = lambda n, s, d=F32: nc.dram_tensor(n, list(s), d, kind="ExternalInput").ap()
    dt_out = lambda n, s: nc.dram_tensor(n, list(s), F32, kind="ExternalOutput").ap()
    x_p = dt_in("x_p", [2, SEQ, D])
    x_s = dt_in("x_s", [128, D])
    meta = dt_in("meta", [16, D])
    ck = dt_in("ck", [L, 16, 128, 128])
    cv = dt_in("cv", [L, 16, 128, 128])
    stp = dt_in("stp", [L, 16, 15, 512])
    w_in = dt_in("w_in", [L, D, 1280])
    w_out = dt_in("w_out", [L, D, D])
    pool_w = dt_in("pool_w", [L, 4, 128, 128])
    w_up = dt_in("w_up", [L, D, 4096])
    w_dn = dt_in("w_dn", [L, 4096, D])
    gbc = dt_in("gbc", [L, 2, 128, D])
    pscale = dt_in("pscale", [128, L * 4])
    sinkb = dt_in("sinkb", [128, L * 8])
    sinkc = dt_in("sinkc", [128, L * 2])
    lnf = dt_in("lnf", [128, D])
    ident_d = dt_in("ident", [128, 128], BF16)
    masks_d = dt_in("masks", [128, 7, 256], BF16)
    bmat_d = dt_in("bmat", [128, 28, 128], BF16)
    rope_d = dt_in("rope", [128, 18, 32])
    y_p = dt_out("y_p", [2, SEQ, D])
    y_s = dt_out("y_s", [128, D])
    nkp = dt_out("nkp", [L, 2, 128, 128])
    nvp = dt_out("nvp", [L, 2, 128, 128])
    npp = dt_out("npp", [L, 2, 15, 512])
    nks = dt_out("nks", [L, 16, 128, 128])
    nvs = dt_out("nvs", [L, 16, 128, 128])
    nps = dt_out("nps", [L, 16, 15, 512])

    S = Sched()
    stack = contextlib.ExitStack()
    allbufs = []

    def sb(name, shape, dt=F32):
        t = stack.enter_context(nc.sbuf_tensor("s_" + name, list(shape), dt))
        b = Buf(name)
        allbufs.append(b)
        return t, b

    def sbn(name, n, shape, dt=F32):
        ts, bs = [], []
        for i in range(n):
            t, b = sb("%s%d" % (name, i), shape, dt)
            ts.append(t)
            bs.append(b)
        return ts, bs

    h_t, h_b = sbn("h", 7, [128, D])
    xn2T_t, xn2T_b = sb("xn2T", [128, 8, 7 * 128], BF16)
    xn2T_bt = [Buf("xn2T_%d" % i) for i in range(7)]
    wup_t, wup_b = sbn("wup", 2, [128, 8, 512], BF16)
    wdn_t, wdn_b = sbn("wdn", 2, [128, 4, D], BF16)
    aT_t, aT_b = sbn("aT", 2, [128, 4, 512], BF16)
    r_t, r_b = sbn("relu", 2, [128, 512])
    win_t, win_b = sb("win", [128, 8, 1280], BF16)
    wout_t, wout_b = sb("wout", [128, 8, D], BF16)
    pw_t, pw_b = sb("pw", [128, 4, 128], BF16)
    ident_t, ident_b = sb("ident", [128, 128], BF16)
    masks_t, masks_b = sb("masks", [128, 7, 256], BF16)
    bmat_t, bmat_b = sb("bmat", [128, 28, 128], BF16)
    rope_t, rope_b = sb("rope", [128, 18, 32])
    lnf_t, lnf_b = sb("lnf", [128, D])
    g_t, g_b = sbn("gbc", 2, [128, D])
    pscale_t, pscale_b = sb("pscale", [128, L * 4])
    sinkb_t, sinkb_b = sb("sinkb", [128, L * 8])
    sinkc_t, sinkc_b = sb("sinkc", [128, L * 2])
    eps_t, eps_b = sb("eps", [128, 1])
    xhat_t, xhat_b = sbn("xhat", 2, [128, D], BF16)
    xnT_t, xnT_b = sbn("xnT", 2, [128, 8, 128], BF16)
    qk_t, qk_b = sbn("qk", 2, [128, 640], BF16)
    rt_t, rt_b = sbn("ropet", 2, [128, 10, 16])
    vu_t, vu_b, kT_t, kT_b = [], [], [], []
    for l in range(L):
        a, b = sbn("vu%d_" % l, 4, [128, 640], BF16)
        vu_t.append(a); vu_b.append(b)
        a, b = sbn("kT%d_" % l, 4, [64, 2, 128], BF16)
        kT_t.append(a); kT_b.append(b)
    qT_t, qT_b = sbn("qT", 2, [64, 8, 128], BF16)
    P_t, P_b = sbn("P", 2, [128, 8, 256], BF16)
    PT_t, PT_b = sbn("PT", 2, [128, 8, 2, 128], BF16)
    a_t, a_b = sbn("a", 2, [128, 512], BF16)
    mix_t, mix_b = sbn("mixT", 2, [128, 8, 128], BF16)
    d_t, d_b = sbn("dbf", 2, [128, 4, 128], BF16)
    st_t, st_b = sbn("stat", 8, [128, 64])
    st_b2 = [Buf("statb%d" % i) for i in range(8)]
    st_b3 = [Buf("statc%d" % i) for i in range(8)]
    st_g = [[[Buf("statg%d_%d_%d" % (i, g, k)) for k in range(3)] for g in range(2)] for i in range(8)]
    P_bg = [[Buf("Pg%d_%d" % (i, g)) for g in range(2)] for i in range(2)]
    PT_bg = [[Buf("PTg%d_%d" % (i, g)) for g in range(2)] for i in range(2)]
    a_bg = [[Buf("ag%d_%d" % (i, g)) for g in range(2)] for i in range(2)]
    kc_t, kc_b = sbn("kc", 1, [128, 4, 128], BF16)
    vc_t, vc_b = sbn("vc", 1, [128, 4, 128], BF16)
    uf_t, uf_b = aT_t[1][:, 0:2, :].rearrange("p a c -> p (a c)").bitcast(F32), aT_b[1]
    kvf_t, kvf_b = aT_t[1][:, 2, :].bitcast(F32), aT_b[1]
    kcT_t, kcT_b = r_t[0][0:64, :].bitcast(BF16).rearrange("p (j k t) -> p j k t", j=4, k=2), r_b[0]
    hist_t, hist_b = r_t[1][:, :].bitcast(BF16).rearrange("p (h f) -> p h f", h=2), r_b[1]
    ps_t, ps_b = [], []
    for i in range(8):
        if i == 0:
            ps_all = stack.enter_context(nc.psum_tensor("ps_all", [128, 8, 512], F32))
        ps_t.append(ps_all[:, i, :])
        ps_b.append(Buf("ps%d" % i))
        ps_b[-1].excl = True
        ps_b[-1].bank = i
    out_b = Buf("dram_out")
    allbufs.append(out_b)

    def _nbytes(ap):
        n = 1
        for x in ap.shape:
            n *= x
        return n * (2 if ap.dtype == BF16 else 4)

    def dma(eng, out, in_, reads=(), writes=(), sembuf=None, pwrites=()):
        def fn(e):
            return [e.dma_start(out=out, in_=in_)]
        S.op(eng, fn, reads=reads, writes=writes, pwrites=pwrites, dma=1, sembuf=sembuf,
             cost=(0.1 if eng == "sp" else 1.0), nbytes=max(_nbytes(out), _nbytes(in_)))

    def dmas(eng, pairs, reads=(), writes=(), sembuf=None):
        def fn(e):
            return [e.dma_start(out=o, in_=i) for (o, i) in pairs]
        S.op(eng, fn, reads=reads, writes=writes, dma=len(pairs), sembuf=sembuf,
             cost=0.1 * len(pairs), nbytes=sum(_nbytes(o) for (o, i) in pairs))

    def act(fn, reads, writes, pw=(), cost=None):
        S.op("act", fn, reads=reads, writes=writes, pwrites=pw, cost=cost)

    def dve(fn, reads, writes, pw=(), cost=None):
        S.op("dve", fn, reads=reads, writes=writes, pwrites=pw, cost=cost)

    def pe(fn, reads, writes, pw=(), cost=None):
        S.op("pe", fn, reads=reads, writes=writes, pwrites=pw, cost=cost)

    def bcast_mid(ap2d, n):
        p, x = ap2d.shape
        return ap2d.unsqueeze(1).to_broadcast([p, n, x])

    def bcast_last(ap2d, n):
        p, x = ap2d.shape
        return ap2d.unsqueeze(2).to_broadcast([p, x, n])

    dma("sp", ident_t[:], ident_d, writes=[ident_b], sembuf=ident_b)
    dma("sp", masks_t[:], masks_d, writes=[masks_b], sembuf=masks_b)
    dma("sp", bmat_t[:], bmat_d, writes=[bmat_b], sembuf=bmat_b)
    dma("sp", rope_t[:], rope_d, writes=[rope_b], sembuf=rope_b)
    dma("sp", lnf_t[:], lnf, writes=[lnf_b], sembuf=lnf_b)
    dma("sp", pscale_t[:], pscale, writes=[pscale_b], sembuf=pscale_b)
    dma("sp", sinkb_t[:], sinkb, writes=[sinkb_b], sembuf=sinkb_b)
    dma("sp", sinkc_t[:], sinkc, writes=[sinkc_b], sembuf=sinkc_b)
    dve(lambda e: e.memset(eps_t[:], EPS), [], [eps_b])
    for l in range(L):
        dmas("sp", [(nks[l, :, 0:120, :], ck[l, :, 8:128, :]), (nvs[l, :, 0:120, :], cv[l, :, 8:128, :]),
                    (nps[l, :, 0:7, :], stp[l, :, 8:15, :])], writes=[], sembuf=out_b)

    def load_att_weights(l):
        dma("pool", win_t[:], w_in[l].rearrange("(k p) n -> p k n", p=128), writes=[win_b], sembuf=win_b)
        dma("pool", wout_t[:], w_out[l].rearrange("(k p) n -> p k n", p=128), writes=[wout_b], sembuf=wout_b)
        dma("pool", pw_t[:], pool_w[l].rearrange("g c e -> c g e"), writes=[pw_b], sembuf=pw_b)

    def load_mlp_piece(l, p):
        s = p % 2
        dma("pool", wup_t[s][:], w_up[l][:, p * 512:(p + 1) * 512].rearrange("(k p) n -> p k n", p=128),
            writes=[wup_b[s]], sembuf=wup_b[s])
        dma("pool", wdn_t[s][:], w_dn[l][p * 512:(p + 1) * 512, :].rearrange("(c p) n -> p c n", p=128),
            writes=[wdn_b[s]], sembuf=wdn_b[s])

    def tile_info(ti):
        if ti == 0:
            return dict(kind="meta", rope=0, mask=2)
        if ti == 33:
            return dict(kind="sample", rope=17)
        seq, r = divmod(ti - 1, 16)
        return dict(kind="prompt", seq=seq, r=r, rope=1 + r, mask=(1 if r == 0 else 0))

    def ring_slot(ti):
        if ti == 0:
            return 3
        return ti % 3

    def prev_slot(ti):
        info = tile_info(ti)
        if info["kind"] == "prompt":
            return 3 if info["r"] == 0 else ring_slot(ti - 1)
        return None

    cnt = {"stat0": 0, "stat1": 0, "i": 0}

    def stat(par):
        k = "stat%d" % par
        i = 4 * par + cnt[k] % 4
        cnt[k] += 1
        return st_t[i], st_b[i], st_b2[i], st_b3[i]

    def stat_g(par):
        k = "stat%d" % par
        i = 4 * par + cnt[k] % 4
        cnt[k] += 1
        return st_t[i], st_g[i]


    def norm_T(hs, gi, dstT, dstT_b, par, bank, xh=None, xhb=None):
        sT, sB, _, _ = stat(par)
        if xh is None:
            xh, xhb = xhat_t[par], xhat_b[par]
        act(lambda e: e.activation(out=xh[:], in_=h_t[hs][:], func=AF.Square, accum_out=sT[:, 0:1]),
            [h_b[hs]], [xhb, sB], cost=1.15)
        act(lambda e: e.activation(out=sT[:, 1:2], in_=sT[:, 0:1], func=AF.Ln, bias=eps_t[:, 0:1], scale=1.0 / D),
            [sB, eps_b], [sB], cost=0.3)
        act(lambda e: e.activation(out=sT[:, 2:3], in_=sT[:, 1:2], func=AF.Exp, scale=-0.5), [sB], [sB], cost=0.25)
        dve(lambda e: e.scalar_tensor_tensor(out=xh[:], in0=h_t[hs][:], scalar=sT[:, 2:3], in1=g_t[gi][:], op0=ALU.mult, op1=ALU.mult),
            [h_b[hs], sB, g_b[gi]], [xhb], cost=1.2)
        trps = ps_t[bank].bitcast(BF16)

        def tr(e):
            r = None
            for k in range(8):
                r = e.transpose(out=trps[:, k * 128:(k + 1) * 128], in_=xh[:, k * 128:(k + 1) * 128], identity=ident_t[:])
            return r
        pe(tr, [xhb, ident_b], [ps_b[bank]], cost=0.5)
        act(lambda e: e.activation(out=dstT[:, 0:4, :], in_=trps[:, 0:512].rearrange("p (k t) -> p k t", k=4), func=AF.Copy),
            [ps_b[bank]], [], pw=[dstT_b], cost=0.5)
        dve(lambda e: e.tensor_copy(out=dstT[:, 4:8, :], in_=trps[:, 512:1024].rearrange("p (k t) -> p k t", k=4)),
            [ps_b[bank]], [], pw=[dstT_b], cost=0.4)

    def att_tile(l, ti, hs, par):
        info = tile_info(ti)
        kind = info["kind"]
        B0 = 4 * par
        rs_ = ring_slot(ti)
        vu_own, vu_own_b = vu_t[l][rs_], vu_b[l][rs_]
        kT_own, kT_own_b = kT_t[l][rs_], kT_b[l][rs_]
        xnT, xnTb = xnT_t[par], xnT_b[par]
        norm_T(hs, 0, xnT, xnTb, par, B0 + 3)
        yield
        bq, bk_, bu = B0 + 1, B0 + 2, B0 + 3

        def inproj(e):
            r = None
            for k in range(8):
                for bank, (c0, c1) in ((bq, (0, 512)), (bk_, (512, 1024)), (bu, (1024, 1280))):
                    r = e.matmul(ps_t[bank][:, 0:c1 - c0], lhsT=xnT[:, k, :], rhs=win_t[:, k, c0:c1],
                                 start=(k == 0), stop=(k == 7))
            return r
        pe(inproj, [xnTb, win_b], [ps_b[bq], ps_b[bk_], ps_b[bu]], cost=3.9)
        qk, qkb = qk_t[par], qk_b[par]
        act(lambda e: e.activation(out=qk[:, 0:512], in_=ps_t[bq][:, 0:512], func=AF.Copy), [ps_b[bq]], [], pw=[qkb])
        act(lambda e: e.activation(out=qk[:, 512:640], in_=ps_t[bk_][:, 0:128], func=AF.Copy), [ps_b[bk_]], [], pw=[qkb])
        act(lambda e: e.activation(out=vu_own[:, 0:384], in_=ps_t[bk_][:, 128:512], func=AF.Copy), [ps_b[bk_]], [], pw=[vu_own_b])
        act(lambda e: e.activation(out=vu_own[:, 384:640], in_=ps_t[bu][:, 0:256], func=AF.Copy), [ps_b[bu]], [], pw=[vu_own_b])
        ri = info["rope"]
        cc = bcast_mid(rope_t[:, ri, 0:16], 10)
        ns = bcast_mid(rope_t[:, ri, 16:24], 10)
        psn = bcast_mid(rope_t[:, ri, 24:32], 10)
        x3 = ps_all[:, bq:bq + 2, :].rearrange("p a c -> p (a c)")[:, 0:640].rearrange("p (h d) -> p h d", h=10)
        qkv3 = qk[:].rearrange("p (h d) -> p h d", h=10)
        tA, tB = rt_t
        dve(lambda e: e.tensor_tensor(out=tA[:], in0=x3[:, :, 0:16], in1=cc, op=ALU.mult), [ps_b[bq], ps_b[bk_], rope_b], [rt_b[0]])
        dve(lambda e: e.tensor_tensor(out=tB[:, :, 0:8], in0=x3[:, :, 8:16], in1=ns, op=ALU.mult), [ps_b[bq], ps_b[bk_], rope_b], [], pw=[rt_b[1]])
        dve(lambda e: e.tensor_tensor(out=tB[:, :, 8:16], in0=x3[:, :, 0:8], in1=psn, op=ALU.mult), [ps_b[bq], ps_b[bk_], rope_b], [], pw=[rt_b[1]])
        dve(lambda e: e.tensor_tensor(out=qkv3[:, :, 0:16], in0=tA[:], in1=tB[:], op=ALU.add), [rt_b[0], rt_b[1]], [qkb])
        is_out = (kind == "sample") or (kind == "prompt" and info["r"] == 15)
        if is_out:
            act(lambda e: e.activation(out=kvf_t[:], in_=ps_t[bk_][:, 0:256], func=AF.Copy), [ps_b[bk_]], [kvf_b])
            act(lambda e: e.activation(out=uf_t[:, 0:256], in_=ps_t[bk_][:, 256:512], func=AF.Copy), [ps_b[bk_]], [], pw=[uf_b])
            act(lambda e: e.activation(out=uf_t[:, 256:512], in_=ps_t[bu][:, 0:256], func=AF.Copy), [ps_b[bu]], [], pw=[uf_b])
            kf3 = kvf_t[:, 0:128].rearrange("p (h d) -> p h d", h=2)
            dve(lambda e: e.tensor_tensor(out=kf3[:, :, 0:16], in0=tA[:, 8:10, :], in1=tB[:, 8:10, :], op=ALU.add),
                [rt_b[0], rt_b[1]], [kvf_b])
            if kind == "prompt":
                sq = info["seq"]
                dmas("sp", [(nkp[l, sq], kvf_t[:, 0:128]), (nvp[l, sq], kvf_t[:, 128:256])], reads=[kvf_b], sembuf=kvf_b)
                dma("sp", npp[l, sq], uf_t[113:128, :], reads=[uf_b], sembuf=uf_b)
            else:
                pairs = []
                for s_ in range(16):
                    pairs.append((nks[l, s_, 120:128, :], kvf_t[8 * s_:8 * s_ + 8, 0:128]))
                    pairs.append((nvs[l, s_, 120:128, :], kvf_t[8 * s_:8 * s_ + 8, 128:256]))
                dmas("sp", pairs, reads=[kvf_b], sembuf=kvf_b)
                dmas("sp", [(nps[l, s_, 7:15, :], uf_t[8 * s_:8 * s_ + 8, :]) for s_ in range(16)], reads=[uf_b], sembuf=uf_b)
        yield
        bqT, bkT, bdT, bpm = B0 + 0, B0 + 1, B0 + 2, B0 + 3
        qTps = ps_t[bqT][:].bitcast(BF16)
        kTps = ps_t[bkT][:].bitcast(BF16)

        def qktr(e):
            r = None
            for hh in range(8):
                r = e.transpose(out=qTps[0:64, hh * 128:(hh + 1) * 128], in_=qk[:, hh * 64:(hh + 1) * 64], identity=ident_t[:])
            for kv in range(2):
                r = e.transpose(out=kTps[0:64, kv * 128:(kv + 1) * 128], in_=qk[:, 512 + kv * 64:512 + (kv + 1) * 64], identity=ident_t[:])
            return r
        pe(qktr, [qkb, ident_b], [ps_b[bqT], ps_b[bkT]], cost=0.8)
        qT, qTb = qT_t[par], qT_b[par]
        if kind != "sample":
            act(lambda e: e.activation(out=qT[:].rearrange("p h t -> p (h t)"), in_=qTps[0:64, :], func=AF.Copy), [ps_b[bqT]], [qTb])
        else:
            for g in range(2):
                src = qTps[0:64, g * 512:(g + 1) * 512].rearrange("p (h s i) -> p h s i", h=4, s=16)
                dst = qT[:].rearrange("p h t -> p (h t)")[:, g * 512:(g + 1) * 512].rearrange("p (s h i) -> p h s i", s=16, h=4)
                dve(lambda e, src=src, dst=dst: e.tensor_copy(out=dst, in_=src), [ps_b[bqT]], [qTb])
        dve(lambda e: e.tensor_copy(out=kT_own[:].rearrange("p h t -> p (h t)"), in_=kTps[0:64, 0:256]), [ps_b[bkT]], [kT_own_b])
        mix, mixb = mix_t[par], mix_b[par]
        dbf, dbfb = d_t[par], d_b[par]
        if kind == "sample":
            for hh in range(2):
                dma("pool", hist_t[0:120, hh, :], stp[l, 8 * hh:8 * hh + 8].rearrange("s r f -> (s r) f"),
                    writes=[hist_b], sembuf=hist_b)

        def poolB(e):
            r = None
            for g in range(4):
                uo = vu_own[:, 128 + g * 128:128 + (g + 1) * 128]
                dst = ps_t[bdT][:, g * 128:(g + 1) * 128]
                if kind == "meta":
                    e.matmul(dst, lhsT=uo, rhs=bmat_t[:, 8 + g, :], start=True, stop=False)
                    r = e.matmul(dst, lhsT=uo, rhs=bmat_t[:, 24 + g, :], start=False, stop=True)
                elif kind == "prompt":
                    up = vu_t[l][prev_slot(ti)][:, 128 + g * 128:128 + (g + 1) * 128]
                    e.matmul(dst, lhsT=up, rhs=bmat_t[:, 4 + g, :], start=True, stop=False)
                    r = e.matmul(dst, lhsT=uo, rhs=bmat_t[:, 0 + g, :], start=False, stop=True)
                else:
                    for hh in range(2):
                        e.matmul(dst, lhsT=hist_t[0:120, hh, g * 128:(g + 1) * 128], rhs=bmat_t[0:120, 16 + 2 * g + hh, :],
                                 start=(hh == 0), stop=False)
                    r = e.matmul(dst, lhsT=uo, rhs=bmat_t[:, 12 + g, :], start=False, stop=True)
            return r
        rd = [vu_own_b, bmat_b]
        if kind == "prompt":
            rd.append(vu_b[l][prev_slot(ti)])
        if kind == "sample":
            rd.append(hist_b)
        pe(poolB, rd, [ps_b[bdT]])
        act(lambda e: e.activation(out=dbf[:].rearrange("p g t -> p (g t)"), in_=ps_t[bdT][:], func=AF.Copy), [ps_b[bdT]], [dbfb])
        yield

        def poolW(e):
            r = None
            for g in range(4):
                r = e.matmul(ps_t[bpm][:, g * 128:(g + 1) * 128], lhsT=pw_t[:, g, :], rhs=dbf[:, g, :], start=True, stop=True)
            return r
        pe(poolW, [dbfb, pw_b], [ps_b[bpm]], cost=0.3)
        dve(lambda e: e.tensor_tensor(out=mix[:, 4:8, :], in0=ps_t[bpm][:].rearrange("p (g t) -> p g t", g=4),
                                      in1=bcast_last(pscale_t[:, l * 4:(l + 1) * 4], 128), op=ALU.mult),
            [ps_b[bpm], pscale_b], [], pw=[mixb])
        yield
        if kind != "sample":
            ps_slot = prev_slot(ti)
            if ps_slot is None:
                kT_prev, kT_prev_b, vu_prev, vu_prev_b = kT_own, kT_own_b, vu_own, vu_own_b
            else:
                kT_prev, kT_prev_b = kT_t[l][ps_slot], kT_b[l][ps_slot]
                vu_prev, vu_prev_b = vu_t[l][ps_slot], vu_b[l][ps_slot]
            P, PT = P_t[par], PT_t[par]
            Pg, PTg, ag = P_bg[par], PT_bg[par], a_bg[par]
            asb = a_t[par]
            sT, sG = stat_g(par)
            sbanks = ((B0 + 0, B0 + 1), (B0 + 2, B0 + 3))
            mask2d = masks_t[:, info["mask"], :]

            def st_scores(g):
                ba, bb_ = sbanks[g]
                Ba = sG[g][0]

                def scores(e):
                    r = None
                    for h4 in range(4):
                        bank = ps_t[ba] if h4 < 2 else ps_t[bb_]
                        o = (h4 % 2) * 256
                        e.matmul(bank[:, o:o + 128], lhsT=qT[:, g * 4 + h4, :], rhs=kT_prev[:, g, :], start=True, stop=False)
                        e.matmul(bank[:, o + 128:o + 256], lhsT=qT[:, g * 4 + h4, :], rhs=kT_own[:, g, :], start=False, stop=False)
                        r = e.matmul(bank[:, o:o + 256], lhsT=ident_t[:], rhs=mask2d, start=False, stop=True)
                    return r
                pe(scores, [qTb, kT_prev_b, kT_own_b, ident_b, masks_b], [ps_b[ba], ps_b[bb_]], cost=1.8)
                for hf, bk in ((0, ba), (1, bb_)):
                    v3 = ps_t[bk][:].rearrange("p (h s) -> p h s", h=2)
                    cc0 = g * 4 + hf * 2
                    dve(lambda e, v3=v3, cc0=cc0: e.tensor_reduce(out=sT[:, cc0:cc0 + 2], in_=v3, axis=AX.X, op=ALU.max),
                        [ps_b[bk]], [], pw=[Ba])
                c0, c1 = g * 4, g * 4 + 4
                dve(lambda e: e.scalar_tensor_tensor(out=sT[:, 8 + c0:8 + c1], in0=sT[:, c0:c1], scalar=SCALE, in1=sinkb_t[:, l * 8 + c0:l * 8 + c1],
                                                     op0=ALU.mult, op1=ALU.max),
                    [Ba, sinkb_b], [Ba], cost=0.2)
                dve(lambda e: e.tensor_scalar(out=sT[:, 16 + c0:16 + c1], in0=sT[:, 8 + c0:8 + c1], scalar1=-1.0, scalar2=None, op0=ALU.mult),
                    [Ba], [Ba], cost=0.2)

            def st_exp(g):
                ba, bb_ = sbanks[g]
                Ba, Bb, Bc = sG[g]
                c0, c1 = g * 4, g * 4 + 4
                for h4 in range(4):
                    bk = ba if h4 < 2 else bb_
                    o = (h4 % 2) * 256
                    hh = g * 4 + h4
                    act(lambda e, bk=bk, o=o, hh=hh: e.activation(out=P[:, hh, :], in_=ps_t[bk][:, o:o + 256], func=AF.Exp,
                                                                   bias=sT[:, 16 + hh:17 + hh], scale=SCALE,
                                                                   accum_out=sT[:, 24 + hh:25 + hh]),
                        [ps_b[bk], Ba], [], pw=[Pg[g], Bb], cost=0.47)
                dve(lambda e: e.tensor_tensor(out=sT[:, 32 + c0:32 + c1], in0=sinkb_t[:, l * 8 + c0:l * 8 + c1], in1=sT[:, 16 + c0:16 + c1], op=ALU.add),
                    [Ba, sinkb_b], [Bc], cost=0.15)
                act(lambda e: e.activation(out=sT[:, 40 + c0:40 + c1], in_=sT[:, 32 + c0:32 + c1], func=AF.Exp), [Bc], [Bc], cost=0.25)
                dve(lambda e: e.tensor_tensor(out=sT[:, 48 + c0:48 + c1], in0=sT[:, 40 + c0:40 + c1], in1=sT[:, 24 + c0:24 + c1], op=ALU.add), [Bc, Bb], [Bc], cost=0.15)
                dve(lambda e: e.reciprocal(out=sT[:, 56 + c0:56 + c1], in_=sT[:, 48 + c0:48 + c1]), [Bc], [Bc], cost=0.2)

            def st_pt(g):
                bpt = sbanks[g][0]
                ptps = ps_t[bpt].bitcast(BF16)

                def ptr(e):
                    r = None
                    for h4 in range(4):
                        for hf in range(2):
                            c = (h4 * 2 + hf) * 128
                            r = e.transpose(out=ptps[:, c:c + 128], in_=P[:, g * 4 + h4, hf * 128:(hf + 1) * 128], identity=ident_t[:])
                    return r
                pe(ptr, [Pg[g], ident_b], [ps_b[bpt]], cost=0.45)
                dst = PT[:, g * 4:(g + 1) * 4, :, :].rearrange("p h f t -> p (h f t)")
                if g == 0:
                    act(lambda e: e.activation(out=dst, in_=ptps[:, :], func=AF.Copy), [ps_b[bpt]], [PTg[g]])
                else:
                    dve(lambda e: e.tensor_copy(out=dst, in_=ptps[:, :]), [ps_b[bpt]], [PTg[g]])

            def st_pv(g):
                bpv = sbanks[g][1]
                Bc = sG[g][2]

                def pv(e):
                    r = None
                    for h4 in range(4):
                        hh = g * 4 + h4
                        dst = ps_t[bpv][:, h4 * 64:(h4 + 1) * 64]
                        e.matmul(dst, lhsT=PT[:, hh, 0, :], rhs=vu_prev[:, g * 64:(g + 1) * 64], start=True, stop=False)
                        r = e.matmul(dst, lhsT=PT[:, hh, 1, :], rhs=vu_own[:, g * 64:(g + 1) * 64], start=False, stop=True)
                    return r
                pe(pv, [PTg[g], vu_prev_b, vu_own_b], [ps_b[bpv]], cost=0.4)
                dve(lambda e: e.tensor_tensor(out=asb[:, g * 256:(g + 1) * 256].rearrange("p (h d) -> p h d", h=4),
                                              in0=ps_t[bpv][:, 0:256].rearrange("p (h d) -> p h d", h=4),
                                              in1=bcast_last(sT[:, 56 + g * 4:60 + g * 4], 64), op=ALU.mult), [ps_b[bpv], Bc], [ag[g]])

            def st_at(g):
                baT = sbanks[g][0]
                aTps = ps_t[baT].bitcast(BF16)

                def atr(e):
                    r = None
                    for c in range(2):
                        cc_ = g * 2 + c
                        r = e.transpose(out=aTps[:, c * 128:(c + 1) * 128], in_=asb[:, cc_ * 128:(cc_ + 1) * 128], identity=ident_t[:])
                    return r
                pe(atr, [ag[g], ident_b], [ps_b[baT]], cost=0.15)
                act(lambda e: e.activation(out=mix[:, g * 2:g * 2 + 2, :].rearrange("p c t -> p (c t)"), in_=aTps[:, 0:256], func=AF.Copy),
                    [ps_b[baT]], [], pw=[mixb])

            st_scores(0)
            yield
            st_scores(1)
            st_exp(0)
            yield
            st_pt(0)
            st_exp(1)
            yield
            st_pv(0)
            st_pt(1)
            yield
            st_at(0)
            st_pv(1)
            yield
            st_at(1)
            yield
        else:
            for T in range(4):
                kp = 0
                dma("pool", kc_t[kp][:], ck[l, 4 * T:4 * T + 4].rearrange("s k f -> k s f"), writes=[kc_b[kp]], sembuf=kc_b[kp])
                dma("pool", vc_t[kp][:], cv[l, 4 * T:4 * T + 4].rearrange("s k f -> k s f"), writes=[vc_b[kp]], sembuf=vc_b[kp])
                bkc = B0 + 0
                kcTps = ps_t[bkc][:].bitcast(BF16)

                def kctr(e, kp=kp, kcTps=kcTps):
                    r = None
                    for j in range(4):
                        for kv in range(2):
                            c = (j * 2 + kv) * 128
                            r = e.transpose(out=kcTps[0:64, c:c + 128], in_=kc_t[kp][:, j, kv * 64:(kv + 1) * 64], identity=ident_t[:])
                    return r
                pe(kctr, [kc_b[kp], ident_b], [ps_b[bkc]])
                act(lambda e, kcTps=kcTps: e.activation(out=kcT_t.rearrange("p j k t -> p (j k t)"), in_=kcTps[0:64, :], func=AF.Copy),
                    [ps_b[bkc]], [kcT_b])
                yield
                for g in range(2):
                    sT, sB, sB2, sB3 = stat(par)
                    pp = g
                    P, Pb = P_t[par], P_bg[par][0]
                    PT, PTb = PT_t[par], PT_bg[par][0]
                    sbk = B0 + 1 + g

                    def sscores(e, T=T, g=g, sbk=sbk):
                        r = None
                        for j in range(4):
                            s = 4 * T + j
                            lt = qT[:].rearrange("p h t -> p (h t)")[:, (g * 16 + s) * 32:(g * 16 + s + 1) * 32]
                            e.matmul(ps_t[sbk][32 * j:32 * j + 32, 0:128], lhsT=lt, rhs=kcT_t[:, j, g, :], start=True, stop=True,
                                     tile_position=(0, 32 * j))
                            r = e.matmul(ps_t[sbk][32 * j:32 * j + 32, 128:256], lhsT=lt, rhs=kT_own[:, g, :], start=True, stop=True,
                                         tile_position=(0, 32 * j))
                        return r
                    pe(sscores, [qTb, kcT_b, kT_own_b], [ps_b[sbk]])
                    sv = ps_t[sbk][:, 0:256]
                    dve(lambda e, sv=sv, T=T: e.tensor_tensor(out=sv, in0=sv, in1=masks_t[:, 3 + T, :], op=ALU.add),
                        [ps_b[sbk], masks_b], [ps_b[sbk]])
                    dve(lambda e, sv=sv, sT=sT: e.tensor_reduce(out=sT[:, 0:1], in_=sv, axis=AX.X, op=ALU.max), [ps_b[sbk]], [sB])
                    sc = sinkc_t[:, l * 2 + g:l * 2 + g + 1]
                    dve(lambda e, sT=sT, sc=sc: e.scalar_tensor_tensor(out=sT[:, 1:2], in0=sT[:, 0:1], scalar=SCALE, in1=sc,
                                                                       op0=ALU.mult, op1=ALU.max), [sB, sinkc_b], [sB])
                    dve(lambda e, sT=sT: e.tensor_scalar(out=sT[:, 2:3], in0=sT[:, 1:2], scalar1=-1.0, scalar2=None, op0=ALU.mult), [sB], [sB])
                    yield
                    act(lambda e, sv=sv, sT=sT, P=P, g=g: e.activation(out=P[:, g, :], in_=sv, func=AF.Exp, bias=sT[:, 2:3], scale=SCALE,
                                                                       accum_out=sT[:, 3:4]), [ps_b[sbk], sB], [sB2], pw=[Pb])
                    dve(lambda e, sT=sT, sc=sc: e.tensor_tensor(out=sT[:, 4:5], in0=sc, in1=sT[:, 2:3], op=ALU.add), [sB, sinkc_b], [sB3])
                    act(lambda e, sT=sT: e.activation(out=sT[:, 5:6], in_=sT[:, 4:5], func=AF.Exp), [sB3], [sB3])
                    dve(lambda e, sT=sT: e.tensor_tensor(out=sT[:, 6:7], in0=sT[:, 5:6], in1=sT[:, 3:4], op=ALU.add), [sB3, sB2], [sB3])
                    dve(lambda e, sT=sT: e.reciprocal(out=sT[:, 7:8], in_=sT[:, 6:7]), [sB3], [sB3])
                    yield
                    bpt = B0 + 3
                    ptps = ps_t[bpt][:].bitcast(BF16)

                    def sptr(e, P=P, ptps=ptps, g=g):
                        e.transpose(out=ptps[:, 0:128], in_=P[:, g, 0:128], identity=ident_t[:])
                        return e.transpose(out=ptps[:, 128:256], in_=P[:, g, 128:256], identity=ident_t[:])
                    pe(sptr, [Pb, ident_b], [ps_b[bpt]])
                    act(lambda e, PT=PT, ptps=ptps, g=g: e.activation(out=PT[:, g, :, :].rearrange("p f t -> p (f t)"), in_=ptps[:, 0:256], func=AF.Copy),
                        [ps_b[bpt]], [], pw=[PTb])
                    yield
                    bsv = B0 + 0

                    def spv(e, PT=PT, g=g, kp=kp, bsv=bsv):
                        r = None
                        for j in range(4):
                            dst = ps_t[bsv][32 * j:32 * j + 32, 0:64]
                            e.matmul(dst, lhsT=PT[:, g, 0, 32 * j:32 * j + 32], rhs=vc_t[kp][:, j, g * 64:(g + 1) * 64], start=True, stop=False,
                                     tile_position=(0, 32 * j))
                            r = e.matmul(dst, lhsT=PT[:, g, 1, 32 * j:32 * j + 32], rhs=vu_own[:, g * 64:(g + 1) * 64], start=False, stop=True,
                                         tile_position=(0, 32 * j))
                        return r
                    pe(spv, [PTb, vc_b[kp], vu_own_b], [ps_b[bsv]])
                    asb, asbb = a_t[par], a_bg[par][0]
                    dve(lambda e, asb=asb, sT=sT, g=g, bsv=bsv: e.tensor_scalar(out=asb[:, g * 64:(g + 1) * 64], in0=ps_t[bsv][:, 0:64], scalar1=sT[:, 7:8], scalar2=None, op0=ALU.mult),
                        [ps_b[bsv], sB3], [], pw=[asbb])
                    yield
                    aTs = ps_t[bpt][:].bitcast(BF16)

                    def satr(e, asb=asb, aTs=aTs, g=g):
                        e.transpose(out=aTs[0:64, 0:128], in_=asb[:, g * 64:(g + 1) * 64], identity=ident_t[:])
                        return e.transpose(out=aTs[64:128, 0:128], in_=asb[:, g * 64:(g + 1) * 64], identity=ident_t[:], tile_position=(0, 64))
                    pe(satr, [asbb, ident_b], [ps_b[bpt]])
                    for h4 in range(4):
                        p0 = 0 if h4 % 2 == 0 else 64
                        c = g * 2 + h4 // 2
                        src = aTs[p0:p0 + 64, 0:128].rearrange("p (j h i) -> p j h i", j=4, h=4)[:, :, h4, :]
                        dst = mix[p0:p0 + 64, c, 32 * T:32 * T + 32].rearrange("p (j i) -> p j i", j=4)
                        if h4 < 2:
                            act(lambda e, src=src, dst=dst: e.activation(out=dst, in_=src, func=AF.Copy), [ps_b[bpt]], [], pw=[mixb])
                        else:
                            dve(lambda e, src=src, dst=dst: e.tensor_copy(out=dst, in_=src), [ps_b[bpt]], [], pw=[mixb])
                    yield
        bo = B0

        def oproj(e):
            r = None
            for c in range(8):
                for hf in range(2):
                    r = e.matmul(ps_t[bo + hf][:], lhsT=mix[:, c, :], rhs=wout_t[:, c, hf * 512:(hf + 1) * 512], start=(c == 0), stop=(c == 7))
            return r
        pe(oproj, [mixb, wout_b], [ps_b[bo], ps_b[bo + 1]], cost=3.6)
        for hf in range(2):
            dve(lambda e, hf=hf: e.tensor_tensor(out=h_t[hs][:, hf * 512:(hf + 1) * 512], in0=h_t[hs][:, hf * 512:(hf + 1) * 512],
                                                 in1=ps_t[bo + hf][:], op=ALU.add), [h_b[hs], ps_b[bo + hf]], [h_b[hs]])
        yield
        norm_T(hs, 1, xn2T_t[:, :, hs * 128:(hs + 1) * 128], xn2T_bt[hs], par, B0 + 2,
               xh=P_t[par][:, 0:4, :].rearrange("p h s -> p (h s)"), xhb=P_bg[par][0])

    def att_phase(l, tiles):
        LAG = 3
        gens = [att_tile(l, ti, hs, hs % 2) for hs, ti in enumerate(tiles)]
        inflight, progress, nxt = [], {}, 0
        while nxt < len(gens) or inflight:
            if nxt < len(gens) and len(inflight) < 2 and (not inflight or progress[inflight[-1]] >= LAG):
                inflight.append(nxt)
                progress[nxt] = 0
                nxt += 1
            for g in list(inflight):
                try:
                    next(gens[g])
                    progress[g] += 1
                except StopIteration:
                    inflight.remove(g)

    def mlp_phase(l, ntiles):
        nsub = (ntiles + 3) // 4
        base, rem = divmod(ntiles, nsub)
        subs, a0 = [], 0
        for i_ in range(nsub):
            a1 = a0 + base + (1 if i_ < rem else 0)
            subs.append((a0, a1))
            a0 = a1
        upc = [0]
        for p in range(NPIECE):
            s = p % 2
            for (t0, t1) in subs:
                n = (t1 - t0) * 128
                ai = upc[0] % 2
                upc[0] += 1
                aT, aTb = aT_t[ai], aT_b[ai]
                pend = None
                for fc in range(4):
                    bk = fc % 4

                    def up(e, fc=fc, bk=bk, t0=t0, n=n, s=s):
                        r = None
                        for k in range(8):
                            r = e.matmul(ps_t[bk][:, 0:n], lhsT=wup_t[s][:, k, fc * 128:(fc + 1) * 128],
                                         rhs=xn2T_t[:, k, t0 * 128:t0 * 128 + n], start=(k == 0), stop=(k == 7))
                        return r
                    pe(up, [wup_b[s]] + [xn2T_bt[t_] for t_ in range(t0, t1)], [ps_b[bk]], cost=1.75 * n / 512.0)
                    ri = fc % 2
                    act(lambda e, bk=bk, n=n, ri=ri: e.activation(out=r_t[ri][:, 0:n], in_=ps_t[bk][:, 0:n], func=AF.Relu),
                        [ps_b[bk]], [r_b[ri]], cost=0.55)
                    if pend is not None:
                        pend()
                    pend = (lambda ri=ri, n=n, fc=fc, aT=aT, aTb=aTb:
                            act(lambda e: e.activation(out=aT[:, fc, 0:n], in_=r_t[ri][:, 0:n], func=AF.Square), [r_b[ri]], [], pw=[aTb], cost=0.45))
                pend()
                for t in range(t0, t1):
                    db = 4 + 2 * (t % 2)

                    def dn(e, t=t, t0=t0, db=db, aT=aT, s=s):
                        r = None
                        for fc in range(4):
                            for hf in range(2):
                                r = e.matmul(ps_t[db + hf][:], lhsT=aT[:, fc, (t - t0) * 128:(t - t0 + 1) * 128],
                                             rhs=wdn_t[s][:, fc, hf * 512:(hf + 1) * 512], start=(fc == 0), stop=(fc == 3))
                        return r
                    pe(dn, [aTb, wdn_b[s]], [ps_b[db], ps_b[db + 1]], cost=1.75)
                    for hf in range(2):
                        dve(lambda e, t=t, hf=hf, db=db: e.tensor_tensor(out=h_t[t][:, hf * 512:(hf + 1) * 512], in0=h_t[t][:, hf * 512:(hf + 1) * 512],
                                                                          in1=ps_t[db + hf][:], op=ALU.add), [h_b[t], ps_b[db + hf]], [h_b[t]])
            if p + 2 < NPIECE:
                load_mlp_piece(l, p + 2)

    def final_tile(ti, hs, last_mt=False):
        info = tile_info(ti)
        if info["kind"] == "meta":
            return
        par = hs % 2
        sT, sB, _, _ = stat(par)
        xh, xhb = xhat_t[par], xhat_b[par]
        act(lambda e: e.activation(out=xh[:], in_=h_t[hs][:], func=AF.Square, accum_out=sT[:, 0:1]), [h_b[hs]], [xhb, sB])
        act(lambda e: e.activation(out=sT[:, 1:2], in_=sT[:, 0:1], func=AF.Ln, bias=eps_t[:, 0:1], scale=1.0 / D), [sB, eps_b], [sB])
        act(lambda e: e.activation(out=sT[:, 2:3], in_=sT[:, 1:2], func=AF.Exp, scale=-0.5), [sB], [sB])
        if info["kind"] == "prompt":
            dst = y_p[info["seq"], info["r"] * 128:(info["r"] + 1) * 128, :]
        else:
            dst = y_s
        if last_mt:
            dve(lambda e: e.scalar_tensor_tensor(out=h_t[hs][:], in0=h_t[hs][:], scalar=sT[:, 2:3], in1=lnf_t[:], op0=ALU.mult, op1=ALU.mult),
                [h_b[hs], sB, lnf_b], [h_b[hs]], cost=1.2)
            dma("sp", dst, h_t[hs][:], reads=[h_b[hs]], sembuf=h_b[hs])
        else:
            ys = aT_t[par][:].rearrange("p a c -> p (a c)").bitcast(F32)
            dve(lambda e: e.scalar_tensor_tensor(out=ys, in0=h_t[hs][:], scalar=sT[:, 2:3], in1=lnf_t[:], op0=ALU.mult, op1=ALU.mult),
                [h_b[hs], sB, lnf_b], [aT_b[par]], cost=1.2)
            dma("sp", dst, ys, reads=[aT_b[par]], sembuf=aT_b[par])

    load_att_weights(0)
    for mi, tiles in enumerate(MTS):
        for hs, ti in enumerate(tiles):
            info = tile_info(ti)
            if info["kind"] == "meta":
                dve(lambda e, hs=hs: e.memset(h_t[hs][:], 0.0), [], [h_b[hs]])
                dma("sp", h_t[hs][112:128, :], meta, writes=[h_b[hs]], sembuf=h_b[hs])
            elif info["kind"] == "prompt":
                dma("sp", h_t[hs][:], x_p[info["seq"], info["r"] * 128:(info["r"] + 1) * 128, :], writes=[h_b[hs]], sembuf=h_b[hs])
            else:
                dma("sp", h_t[hs][:], x_s, writes=[h_b[hs]], sembuf=h_b[hs])
        for l in range(L):
            S.epoch += 1
            load_mlp_piece(l, 0)
            load_mlp_piece(l, 1)
            for gi in range(2):
                dma("sp", g_t[gi][:], gbc[l, gi], writes=[g_b[gi]], sembuf=g_b[gi])
            att_phase(l, tiles)
            nl, nm = (l + 1, mi) if l + 1 < L else (0, mi + 1)
            if nm < len(MTS):
                load_att_weights(nl)
            mlp_phase(l, len(tiles))
        for hs, ti in enumerate(tiles):
            final_tile(ti, hs, last_mt=(mi == len(MTS) - 1))

    if LIMIT is not None:
        S.ops = S.ops[:LIMIT]
    def dummy_mm(e, bank):
        e.matmul(ps_t[bank][:, 0:512], lhsT=ident_t[:], rhs=bmat_t[:, 0:4, :].rearrange("p a t -> p (a t)"), start=True, stop=True)
    nsem = lower(nc, S, stack, allbufs, reorder=REORDER, dummy=(dummy_mm if DUMMY else None),
                 dummy_deps=list(ident_b.writers) + list(bmat_b.writers))
    stack.close()
    return nc, nsem, len(S.ops)


_CACHE = {}


def kernel(x_prompt, x_sample, cache_k, cache_v, state_pool, meta_tokens, ln1, w_in, attn_sinks,
           pool_w, pool_scale, w_out, ln2, w_up, w_down, ln_f):
    f = lambda a: np.ascontiguousarray(np.asarray(a, dtype=np.float32))
    x_prompt, x_sample, cache_k, cache_v, state_pool = map(f, (x_prompt, x_sample, cache_k, cache_v, state_pool))
    meta_tokens, ln1, w_in, attn_sinks, pool_w, pool_scale, w_out, ln2, w_up, w_down, ln_f = map(
        f, (meta_tokens, ln1, w_in, attn_sinks, pool_w, pool_scale, w_out, ln2, w_up, w_down, ln_f))
    if "nc" not in _CACHE:
        _CACHE["nc"] = build_program()[0]
        _CACHE["consts"] = make_consts()
    nc = _CACHE["nc"]
    consts = _CACHE["consts"]
    gbc = np.broadcast_to(np.stack([ln1, ln2], 1).reshape(L, 2, 1, D), (L, 2, 128, D))
    psc = pool_scale.reshape(L, 4, 128).transpose(2, 0, 1).reshape(128, L * 4)
    skb = np.broadcast_to(attn_sinks.reshape(1, L * 8), (128, L * 8))
    rows = np.arange(128)
    h4 = (rows // 8) % 4
    skc = np.stack([attn_sinks[l, kv * 4 + h4] for l in range(L) for kv in range(2)], axis=1)
    lnfb = np.broadcast_to(ln_f.reshape(1, D), (128, D))
    shared = {
        "meta": meta_tokens, "w_in": w_in, "w_out": w_out, "pool_w": pool_w, "w_up": w_up, "w_dn": w_down,
        "gbc": f(gbc), "pscale": f(psc), "sinkb": f(skb), "sinkc": f(skc), "lnf": f(lnfb),
        "ident": consts["ident"], "masks": consts["masks"], "bmat": consts["bmat"], "rope": consts["rope"],
    }
    in_maps = []
    for c in range(NCORES):
        m = dict(shared)
        m["x_p"] = x_prompt[2 * c:2 * c + 2]
        m["x_s"] = x_sample[16 * c:16 * c + 16].reshape(128, D)
        m["ck"] = f(cache_k[:, 16 * c:16 * c + 16].reshape(L, 16, 128, 128))
        m["cv"] = f(cache_v[:, 16 * c:16 * c + 16].reshape(L, 16, 128, 128))
        m["stp"] = f(state_pool[:, 16 * c:16 * c + 16])
        in_maps.append(m)
    res = run_bass_kernel_spmd(nc, in_maps, core_ids=list(range(NCORES)))
    R = res.results
    y_prompt = np.concatenate([R[c]["y_p"] for c in range(NCORES)], 0)
    y_sample = np.concatenate([R[c]["y_s"].reshape(16, 8, D) for c in range(NCORES)], 0)
    nkp = np.concatenate([R[c]["nkp"] for c in range(NCORES)], 1).reshape(L, 16, 128, 2, 64)
    nvp = np.concatenate([R[c]["nvp"] for c in range(NCORES)], 1).reshape(L, 16, 128, 2, 64)
    npp = np.concatenate([R[c]["npp"] for c in range(NCORES)], 1)
    nks = np.concatenate([R[c]["nks"] for c in range(NCORES)], 1).reshape(L, 128, 128, 2, 64)
    nvs = np.concatenate([R[c]["nvs"] for c in range(NCORES)], 1).reshape(L, 128, 128, 2, 64)
    nps = np.concatenate([R[c]["nps"] for c in range(NCORES)], 1)
    return (y_prompt.astype(np.float32), y_sample.astype(np.float32), nkp, nvp, npp, nks, nvs, nps)
```

```python
import contextlib
import numpy as np
import ml_dtypes
import concourse.bass as bass
import concourse.mybir as mybir
from concourse.bass_utils import run_bass_kernel_spmd

F32 = mybir.dt.float32
BF16 = mybir.dt.bfloat16
AF = mybir.ActivationFunctionType
ALU = mybir.AluOpType
AX = mybir.AxisListType

D = 1024
L = 2
NCORES = 8
SEQ = 2048
PAST = 16384
NEG = -30000.0
SCALE = 0.125
EPS = 1e-5
WINS = (2, 4, 8, 16)
NT = 34
MTS = [list(range(0, 7)), list(range(7, 14)), list(range(14, 21)), list(range(21, 28)), list(range(28, 34))]
NPIECE = 8


class Buf:
    def __init__(self, name):
        self.name = name
        self.writers = []
        self.gen_base = []
        self.readers = []
        self.sem = None
        self.cnt = 0


class Op:
    __slots__ = ("eng", "fn", "deps", "dma", "ndma", "sembuf", "tok", "epoch", "has_dep", "idx", "cost", "start", "end",
                 "in_deps", "out_deps", "bank", "ndummy")


DEFAULT_COST = {"pe": 0.6, "act": 0.5, "dve": 0.5, "pool": 0.3, "sp": 0.1}


class Sched:
    def __init__(self):
        self.ops = []
        self.epoch = 0

    def op(self, eng, fn, reads=(), writes=(), pwrites=(), dma=0, sembuf=None, cost=None, nbytes=0):
        o = Op()
        o.eng, o.fn, o.dma, o.ndma, o.sembuf = eng, fn, bool(dma), dma, sembuf
        o.epoch = self.epoch
        o.has_dep = False
        o.tok = None
        o.idx = len(self.ops)
        o.cost = (cost if cost is not None else DEFAULT_COST[eng], nbytes)
        deps = set()
        in_deps = set()
        o.bank = None
        o.ndummy = 0
        for b in reads:
            if not getattr(b, "excl", False):
                in_deps.update(b.writers)
        for b in list(writes) + list(pwrites):
            if getattr(b, "excl", False) and o.bank is None:
                o.bank = b.bank
        o.in_deps = in_deps
        for b in reads:
            deps.update(b.writers)
            if getattr(b, "excl", False):
                for r_ in b.readers:
                    if self.ops[r_].eng != eng:
                        deps.add(r_)
        out_deps = set()
        for b in writes:
            out_deps.update(b.writers)
            out_deps.update(b.readers)
            out_deps.update(b.gen_base)
        for b in pwrites:
            if b.readers:
                b.gen_base = list(b.readers)
                b.writers = []
                b.readers = []
            out_deps.update(b.gen_base)
            if getattr(b, "excl", False):
                for w_ in b.writers:
                    if self.ops[w_].eng != eng:
                        out_deps.add(w_)
        out_deps.discard(o.idx)
        o.out_deps = out_deps
        deps.update(out_deps)
        deps.discard(o.idx)
        o.deps = sorted(deps)
        self.ops.append(o)
        for b in reads:
            b.readers.append(o.idx)
        for b in writes:
            b.writers = [o.idx]
            b.gen_base = [o.idx]
            b.readers = []
        for b in pwrites:
            b.writers.append(o.idx)
        return o.idx


def _skip_edge(od, o):
    return od.eng == "pe" and o.eng == "pe" and not od.dma and not o.dma


ENGS = ("pe", "act", "dve", "pool", "sp")
DMA_BW = 500e3


PRIO_MODE = "cp"
SLACK = 0.0
PERT_AMP = 0.0
PERT_SEED = 0
HOP_LAT = 0.0
SELF_LAT = 0.0


def list_schedule(ops):
    n = len(ops)
    succ = [[] for _ in range(n)]
    indeg = [0] * n
    for o in ops:
        indeg[o.idx] = len(o.deps)
        for d in o.deps:
            succ[d].append(o.idx)
    cp = [0.0] * n
    for i in range(n - 1, -1, -1):
        o = ops[i]
        c = o.cost[0] + (2.0 + o.cost[1] / DMA_BW if o.dma else 0.0)
        m = 0.0
        for s_ in succ[i]:
            if cp[s_] > m:
                m = cp[s_]
        cp[i] = c + m
    if PERT_AMP > 0.0:
        rs = np.random.RandomState(PERT_SEED)
        pert = 1.0 + PERT_AMP * (rs.rand(n) - 0.5)
        cp = [c_ * p_ for c_, p_ in zip(cp, pert)]
    ready_t = [0.0] * n
    pend = {e: set() for e in ENGS}
    for o in ops:
        if indeg[o.idx] == 0:
            pend[o.eng].add(o.idx)
    eng_free = {e: 0.0 for e in ENGS}
    dma_free = [0.0]
    order = {e: [] for e in ENGS}
    done = 0
    while done < n:
        best = None
        for e in ENGS:
            if not pend[e]:
                continue
            tfree = eng_free[e]
            cand = None
            for idx in pend[e]:
                st = max(tfree, ready_t[idx])
                if PRIO_MODE == "cp":
                    key = (max(st, tfree + SLACK), -cp[idx], idx, st)
                else:
                    key = (st, idx)
                if cand is None or key < cand:
                    cand = key
            k2 = (cand[-1], cand[2]) if PRIO_MODE == "cp" else (cand[0], cand[-1])
            if best is None or k2 < best[0]:
                best = (k2, e)
        (st, idx), e = best
        pend[e].discard(idx)
        o = ops[idx]
        c, nb = o.cost
        if o.dma:
            issue_end = st + c
            xfer_start = max(issue_end, dma_free[0])
            xfer = nb / DMA_BW
            dma_free[0] = xfer_start + xfer
            fin = xfer_start + xfer + 2.0
            eng_free[e] = issue_end
        else:
            fin = st + c
            eng_free[e] = fin
        o.start, o.end = st, fin
        order[e].append(idx)
        done += 1
        for s_ in succ[idx]:
            f2 = fin + (HOP_LAT if ops[s_].eng != e else SELF_LAT)
            if f2 > ready_t[s_]:
                ready_t[s_] = f2
            indeg[s_] -= 1
            if indeg[s_] == 0:
                pend[ops[s_].eng].add(s_)
    return order


DUMMY_MIN_GAP = 0.4
DUMMY_FILL = 0.6
DUMMY_MAX = 2


def lower(nc, sched, stack, final_bufs, reorder=True, dummy=None, dummy_deps=()):
    ops = sched.ops
    for o in ops:
        for d in o.deps:
            od = ops[d]
            if _skip_edge(od, o):
                continue
            od.has_dep = True
    for d in dummy_deps:
        ops[d].has_dep = True
    if reorder:
        order = list_schedule(ops)
    else:
        order = {e: [o.idx for o in ops if o.eng == e] for e in ENGS}
    esem = {}
    ecnt = {}
    nsem = [0]

    def newsem(name):
        nsem[0] += 1
        return stack.enter_context(nc.semaphore(name))

    for e in ENGS:
        for idx in order[e]:
            o = ops[idx]
            if o.dma:
                b = o.sembuf
                if b.sem is None:
                    b.sem = newsem("d_" + b.name)
                b.cnt += 16 * o.ndma
                o.tok = (b.sem, b.cnt)
            elif o.has_dep:
                key = (o.eng, o.epoch)
                if key not in esem:
                    esem[key] = newsem("e_%s_%d" % key)
                    ecnt[key] = 0
                ecnt[key] += 1
                o.tok = (esem[key], ecnt[key])
    finals = [(b.sem, b.cnt) for b in final_bufs if b.sem is not None]

    if dummy is not None:
        prev_end = 0.0
        for idx in order["pe"]:
            o = ops[idx]
            gap = o.start - prev_end
            prev_end = o.end
            if o.bank is not None and gap > DUMMY_MIN_GAP and len(o.in_deps - o.out_deps) > 0:
                o.ndummy = min(DUMMY_MAX, int(gap * DUMMY_FILL / 0.22))

    def emit(engname, eng):
        seen = {}

        def do_waits(o, dl):
            waits = {}
            for d in dl:
                od = ops[d]
                if _skip_edge(od, o):
                    continue
                s, v = od.tok
                k = id(s)
                if k not in waits or waits[k][1] < v:
                    waits[k] = (s, v)
            for k, (s, v) in waits.items():
                if seen.get(k, 0) >= v:
                    continue
                eng.wait_ge(s, v)
                seen[k] = v

        for idx in order[engname]:
            o = ops[idx]
            if engname == "pe" and o.ndummy > 0:
                do_waits(o, list(o.out_deps) + list(dummy_deps))
                for _ in range(o.ndummy):
                    dummy(eng, o.bank)
            do_waits(o, o.deps)
            r = o.fn(eng)
            if o.dma:
                assert len(r) == o.ndma, (len(r), o.ndma)
                for ins in r:
                    ins.then_inc(o.tok[0], 16)
            elif o.tok is not None:
                r.then_inc(o.tok[0], 1)
        if engname == "sp":
            for s, v in finals:
                eng.wait_ge(s, v)

    with nc.Block() as block:
        block.tensor(lambda e: emit("pe", e))
        block.scalar(lambda e: emit("act", e))
        block.vector(lambda e: emit("dve", e))
        block.gpsimd(lambda e: emit("pool", e))
        block.sync(lambda e: emit("sp", e))
    return nsem[0]


def _bf(a):
    return np.asarray(a, dtype=np.float32).astype(ml_dtypes.bfloat16)


def make_consts():
    c = {}
    c["ident"] = _bf(np.eye(128))
    i = np.arange(128)[:, None]
    j = np.arange(128)[None, :]
    masks = np.full((7, 128, 256), NEG, np.float32)
    prev_ok = j >= i
    own_ok = j <= i
    masks[0, :, :128][prev_ok] = 0
    masks[0, :, 128:][own_ok] = 0
    masks[1, :, :128][prev_ok & (j >= 112)] = 0
    masks[1, :, 128:][own_ok] = 0
    masks[2, :, 128:][own_ok & (j >= 112)] = 0
    r = np.arange(128)
    jj = r // 32
    ii = r % 8
    for T in range(4):
        m = masks[3 + T]
        m[:, :128][np.arange(128)[None, :] >= ii[:, None]] = 0
        ks = np.arange(128) // 8
        ki = np.arange(128) % 8
        ok = (ks[None, :] == (4 * T + jj)[:, None]) & (ki[None, :] <= ii[:, None])
        m[:, 128:][ok] = 0
    c["masks"] = _bf(masks.transpose(1, 0, 2))
    B = np.zeros((28, 128, 128), np.float32)
    tp = np.arange(128)[:, None]
    t = np.arange(128)[None, :]
    for g, w in enumerate(WINS):
        own = ((t - tp >= 0) & (t - tp < w)).astype(np.float32) / w - (tp == t)
        B[0 + g] = own
        prv = ((t + 128 - tp > 0) & (t + 128 - tp < w)).astype(np.float32) / w
        B[4 + g] = prv
        cnt = np.minimum(np.maximum(t - 112 + 1, 1), w).astype(np.float32)
        mo = ((t - tp >= 0) & (t - tp < w) & (tp >= 112)).astype(np.float32) / cnt - ((tp == t) & (tp >= 112))
        mo_hi = mo.astype(ml_dtypes.bfloat16).astype(np.float32)
        B[8 + g] = mo_hi
        B[24 + g] = mo - mo_hi
        sp_, ip_ = tp // 8, tp % 8
        s_, i_ = t // 8, t % 8
        so = ((sp_ == s_) & (i_ - ip_ >= 0) & (i_ - ip_ < w)).astype(np.float32) / w - (tp == t)
        B[12 + g] = so
        for hh in range(2):
            hb = np.zeros((128, 128), np.float32)
            rows = np.arange(120)
            sr, rr = rows // 15, rows % 15
            ok = ((sr[:, None] + 8 * hh) == s_) & (rr[:, None] >= 16 + i_ - w)
            hb[:120] = ok.astype(np.float32) / w
            B[16 + 2 * g + hh] = hb
    c["bmat"] = _bf(B.transpose(1, 0, 2))
    inv = (np.float32(500000.0) ** (-np.arange(0, 16, 2, dtype=np.float32) / np.float32(16))).astype(np.float32)
    rope = np.zeros((128, 18, 32), np.float32)
    for ti in range(18):
        if ti == 0:
            pos = np.arange(128) - 112
        elif ti <= 16:
            pos = 16 + 128 * (ti - 1) + np.arange(128)
        else:
            pos = PAST + (np.arange(128) % 8)
        ang = pos.astype(np.float32)[:, None] * inv[None, :]
        cs_, sn_ = np.cos(ang).astype(np.float32), np.sin(ang).astype(np.float32)
        rope[:, ti, 0:8] = cs_
        rope[:, ti, 8:16] = cs_
        rope[:, ti, 16:24] = -sn_
        rope[:, ti, 24:32] = sn_
    c["rope"] = rope
    return c


def build_program(MTS=MTS, LIMIT=None, REORDER=True, DUMMY=True):
    nc = bass.Bass("TRN2", target_bir_lowering=False)
    dt_in = lambda n, s, d=F32: nc.dram_tensor(n, list(s), d, kind="ExternalInput").ap()
    dt_out = lambda n, s: nc.dram_tensor(n, list(s), F32, kind="ExternalOutput").ap()
    x_p = dt_in("x_p", [2, SEQ, D])
    x_s = dt_in("x_s", [128, D])
    meta = dt_in("meta", [16, D])
    ck = dt_in("ck", [L, 16, 128, 128])
    cv = dt_in("cv", [L, 16, 128, 128])
    stp = dt_in("stp", [L, 16, 15, 512])
    w_in = dt_in("w_in", [L, D, 1280])
    w_out = dt_in("w_out", [L, D, D])
    pool_w = dt_in("pool_w", [L, 4, 128, 128])
    w_up = dt_in("w_up", [L, D, 4096])
    w_dn = dt_in("w_dn", [L, 4096, D])
    gbc = dt_in("gbc", [L, 2, 128, D])
    pscale = dt_in("pscale", [128, L * 4])
    sinkb = dt_in("sinkb", [128, L * 8])
    sinkc = dt_in("sinkc", [128, L * 2])
    lnf = dt_in("lnf", [128, D])
    ident_d = dt_in("ident", [128, 128], BF16)
    masks_d = dt_in("masks", [128, 7, 256], BF16)
    bmat_d = dt_in("bmat", [128, 28, 128], BF16)
    rope_d = dt_in("rope", [128, 18, 32])
    y_p = dt_out("y_p", [2, SEQ, D])
    y_s = dt_out("y_s", [128, D])
    nkp = dt_out("nkp", [L, 2, 128, 128])
    nvp = dt_out("nvp", [L, 2, 128, 128])
    npp = dt_out("npp", [L, 2, 15, 512])
    nks = dt_out("nks", [L, 16, 128, 128])
    nvs = dt_out("nvs", [L, 16, 128, 128])
    nps = dt_out("nps", [L, 16, 15, 512])

    S = Sched()
    stack = contextlib.ExitStack()
    allbufs = []

    def sb(name, shape, dt=F32):
        t = stack.enter_context(nc.sbuf_tensor("s_" + name, list(shape), dt))
        b = Buf(name)
        allbufs.append(b)
        return t, b

    def sbn(name, n, shape, dt=F32):
        ts, bs = [], []
        for i in range(n):
            t, b = sb("%s%d" % (name, i), shape, dt)
            ts.append(t)
            bs.append(b)
        return ts, bs

    h_t, h_b = sbn("h", 7, [128, D])
    xn2T_t, xn2T_b = sb("xn2T", [128, 8, 7 * 128], BF16)
    xn2T_bt = [Buf("xn2T_%d" % i) for i in range(7)]
    wup_t, wup_b = sbn("wup", 2, [128, 8, 512], BF16)
    wdn_t, wdn_b = sbn("wdn", 2, [128, 4, D], BF16)
    aT_t, aT_b = sbn("aT", 2, [128, 4, 512], BF16)
    r_t, r_b = sbn("relu", 2, [128, 512])
    win_t, win_b = sb("win", [128, 8, 1280], BF16)
    wout_t, wout_b = sb("wout", [128, 8, D], BF16)
    pw_t, pw_b = sb("pw", [128, 4, 128], BF16)
    ident_t, ident_b = sb("ident", [128, 128], BF16)
    masks_t, masks_b = sb("masks", [128, 7, 256], BF16)
    bmat_t, bmat_b = sb("bmat", [128, 28, 128], BF16)
    rope_t, rope_b = sb("rope", [128, 18, 32])
    lnf_t, lnf_b = sb("lnf", [128, D])
    g_t, g_b = sbn("gbc", 2, [128, D])
    pscale_t, pscale_b = sb("pscale", [128, L * 4])
    sinkb_t, sinkb_b = sb("sinkb", [128, L * 8])
    sinkc_t, sinkc_b = sb("sinkc", [128, L * 2])
    eps_t, eps_b = sb("eps", [128, 1])
    xhat_t, xhat_b = sbn("xhat", 2, [128, D], BF16)
    xnT_t, xnT_b = sbn("xnT", 2, [128, 8, 128], BF16)
    qk_t, qk_b = sbn("qk", 2, [128, 640], BF16)
    rt_t, rt_b = sbn("ropet", 2, [128, 10, 16])
    vu_t, vu_b, kT_t, kT_b = [], [], [], []
    for l in range(L):
        a, b = sbn("vu%d_" % l, 4, [128, 640], BF16)
        vu_t.append(a); vu_b.append(b)
        a, b = sbn("kT%d_" % l, 4, [64, 2, 128], BF16)
        kT_t.append(a); kT_b.append(b)
    qT_t, qT_b = sbn("qT", 2, [64, 8, 128], BF16)
    P_t, P_b = sbn("P", 2, [128, 8, 256], BF16)
    PT_t, PT_b = sbn("PT", 2, [128, 8, 2, 128], BF16)
    a_t, a_b = sbn("a", 2, [128, 512], BF16)
    mix_t, mix_b = sbn("mixT", 2, [128, 8, 128], BF16)
    d_t, d_b = sbn("dbf", 2, [128, 4, 128], BF16)
    st_t, st_b = sbn("stat", 8, [128, 64])
    st_b2 = [Buf("statb%d" % i) for i in range(8)]
    st_b3 = [Buf("statc%d" % i) for i in range(8)]
    st_g = [[[Buf("statg%d_%d_%d" % (i, g, k)) for k in range(3)] for g in range(2)] for i in range(8)]
    P_bg = [[Buf("Pg%d_%d" % (i, g)) for g in range(2)] for i in range(2)]
    PT_bg = [[Buf("PTg%d_%d" % (i, g)) for g in range(2)] for i in range(2)]
    a_bg = [[Buf("ag%d_%d" % (i, g)) for g in range(2)] for i in range(2)]
    kc_t, kc_b = sbn("kc", 1, [128, 4, 128], BF16)
    vc_t, vc_b = sbn("vc", 1, [128, 4, 128], BF16)
    uf_t, uf_b = aT_t[1][:, 0:2, :].rearrange("p a c -> p (a c)").bitcast(F32), aT_b[1]
    kvf_t, kvf_b = aT_t[1][:, 2, :].bitcast(F32), aT_b[1]
    kcT_t, kcT_b = r_t[0][0:64, :].bitcast(BF16).rearrange("p (j k t) -> p j k t", j=4, k=2), r_b[0]
    hist_t, hist_b = r_t[1][:, :].bitcast(BF16).rearrange("p (h f) -> p h f", h=2), r_b[1]
    ps_t, ps_b = [], []
    for i in range(8):
        if i == 0:
            ps_all = stack.enter_context(nc.psum_tensor("ps_all", [128, 8, 512], F32))
        ps_t.append(ps_all[:, i, :])
        ps_b.append(Buf("ps%d" % i))
        ps_b[-1].excl = True
        ps_b[-1].bank = i
    out_b = Buf("dram_out")
    allbufs.append(out_b)

    def _nbytes(ap):
        n = 1
        for x in ap.shape:
            n *= x
        return n * (2 if ap.dtype == BF16 else 4)

    def dma(eng, out, in_, reads=(), writes=(), sembuf=None, pwrites=()):
        def fn(e):
            return [e.dma_start(out=out, in_=in_)]
        S.op(eng, fn, reads=reads, writes=writes, pwrites=pwrites, dma=1, sembuf=sembuf,
             cost=(0.1 if eng == "sp" else 1.0), nbytes=max(_nbytes(out), _nbytes(in_)))

    def dmas(eng, pairs, reads=(), writes=(), sembuf=None):
        def fn(e):
            return [e.dma_start(out=o, in_=i) for (o, i) in pairs]
        S.op(eng, fn, reads=reads, writes=writes, dma=len(pairs), sembuf=sembuf,
             cost=0.1 * len(pairs), nbytes=sum(_nbytes(o) for (o, i) in pairs))

    def act(fn, reads, writes, pw=(), cost=None):
        S.op("act", fn, reads=reads, writes=writes, pwrites=pw, cost=cost)

    def dve(fn, reads, writes, pw=(), cost=None):
        S.op("dve", fn, reads=reads, writes=writes, pwrites=pw, cost=cost)

    def pe(fn, reads, writes, pw=(), cost=None):
        S.op("pe", fn, reads=reads, writes=writes, pwrites=pw, cost=cost)

    def bcast_mid(ap2d, n):
        p, x = ap2d.shape
        return ap2d.unsqueeze(1).to_broadcast([p, n, x])

    def bcast_last(ap2d, n):
        p, x = ap2d.shape
        return ap2d.unsqueeze(2).to_broadcast([p, x, n])

    dma("sp", ident_t[:], ident_d, writes=[ident_b], sembuf=ident_b)
    dma("sp", masks_t[:], masks_d, writes=[masks_b], sembuf=masks_b)
    dma("sp", bmat_t[:], bmat_d, writes=[bmat_b], sembuf=bmat_b)
    dma("sp", rope_t[:], rope_d, writes=[rope_b], sembuf=rope_b)
    dma("sp", lnf_t[:], lnf, writes=[lnf_b], sembuf=lnf_b)
    dma("sp", pscale_t[:], pscale, writes=[pscale_b], sembuf=pscale_b)
    dma("sp", sinkb_t[:], sinkb, writes=[sinkb_b], sembuf=sinkb_b)
    dma("sp", sinkc_t[:], sinkc, writes=[sinkc_b], sembuf=sinkc_b)
    dve(lambda e: e.memset(eps_t[:], EPS), [], [eps_b])
    for l in range(L):
        dmas("sp", [(nks[l, :, 0:120, :], ck[l, :, 8:128, :]), (nvs[l, :, 0:120, :], cv[l, :, 8:128, :]),
                    (nps[l, :, 0:7, :], stp[l, :, 8:15, :])], writes=[], sembuf=out_b)

    def load_att_weights(l):
        dma("pool", win_t[:], w_in[l].rearrange("(k p) n -> p k n", p=128), writes=[win_b], sembuf=win_b)
        dma("pool", wout_t[:], w_out[l].rearrange("(k p) n -> p k n", p=128), writes=[wout_b], sembuf=wout_b)
        dma("pool", pw_t[:], pool_w[l].rearrange("g c e -> c g e"), writes=[pw_b], sembuf=pw_b)

    def load_mlp_piece(l, p):
        s = p % 2
        dma("pool", wup_t[s][:], w_up[l][:, p * 512:(p + 1) * 512].rearrange("(k p) n -> p k n", p=128),
            writes=[wup_b[s]], sembuf=wup_b[s])
        dma("pool", wdn_t[s][:], w_dn[l][p * 512:(p + 1) * 512, :].rearrange("(c p) n -> p c n", p=128),
            writes=[wdn_b[s]], sembuf=wdn_b[s])

    def tile_info(ti):
        if ti == 0:
            return dict(kind="meta", rope=0, mask=2)
        if ti == 33:
            return dict(kind="sample", rope=17)
        seq, r = divmod(ti - 1, 16)
        return dict(kind="prompt", seq=seq, r=r, rope=1 + r, mask=(1 if r == 0 else 0))

    def ring_slot(ti):
        if ti == 0:
            return 3
        return ti % 3

    def prev_slot(ti):
        info = tile_info(ti)
        if info["kind"] == "prompt":
            return 3 if info["r"] == 0 else ring_slot(ti - 1)
        return None

    cnt = {"stat0": 0, "stat1": 0, "i": 0}

    def stat(par):
        k = "stat%d" % par
        i = 4 * par + cnt[k] % 4
        cnt[k] += 1
        return st_t[i], st_b[i], st_b2[i], st_b3[i]

    def stat_g(par):
        k = "stat%d" % par
        i = 4 * par + cnt[k] % 4
        cnt[k] += 1
        return st_t[i], st_g[i]


    def norm_T(hs, gi, dstT, dstT_b, par, bank, xh=None, xhb=None):
        sT, sB, _, _ = stat(par)
        if xh is None:
            xh, xhb = xhat_t[par], xhat_b[par]
        act(lambda e: e.activation(out=xh[:], in_=h_t[hs][:], func=AF.Square, accum_out=sT[:, 0:1]),
            [h_b[hs]], [xhb, sB], cost=1.15)
        act(lambda e: e.activation(out=sT[:, 1:2], in_=sT[:, 0:1], func=AF.Ln, bias=eps_t[:, 0:1], scale=1.0 / D),
            [sB, eps_b], [sB], cost=0.3)
        act(lambda e: e.activation(out=sT[:, 2:3], in_=sT[:, 1:2], func=AF.Exp, scale=-0.5), [sB], [sB], cost=0.25)
        dve(lambda e: e.scalar_tensor_tensor(out=xh[:], in0=h_t[hs][:], scalar=sT[:, 2:3], in1=g_t[gi][:], op0=ALU.mult, op1=ALU.mult),
            [h_b[hs], sB, g_b[gi]], [xhb], cost=1.2)
        trps = ps_t[bank].bitcast(BF16)

        def tr(e):
            r = None
            for k in range(8):
                r = e.transpose(out=trps[:, k * 128:(k + 1) * 128], in_=xh[:, k * 128:(k + 1) * 128], identity=ident_t[:])
            return r
        pe(tr, [xhb, ident_b], [ps_b[bank]], cost=0.5)
        act(lambda e: e.activation(out=dstT[:, 0:4, :], in_=trps[:, 0:512].rearrange("p (k t) -> p k t", k=4), func=AF.Copy),
            [ps_b[bank]], [], pw=[dstT_b], cost=0.5)
        dve(lambda e: e.tensor_copy(out=dstT[:, 4:8, :], in_=trps[:, 512:1024].rearrange("p (k t) -> p k t", k=4)),
            [ps_b[bank]], [], pw=[dstT_b], cost=0.4)

    def att_tile(l, ti, hs, par):
        info = tile_info(ti)
        kind = info["kind"]
        B0 = 4 * par
        rs_ = ring_slot(ti)
        vu_own, vu_own_b = vu_t[l][rs_], vu_b[l][rs_]
        kT_own, kT_own_b = kT_t[l][rs_], kT_b[l][rs_]
        xnT, xnTb = xnT_t[par], xnT_b[par]
        norm_T(hs, 0, xnT, xnTb, par, B0 + 3)
        yield
        bq, bk_, bu = B0 + 1, B0 + 2, B0 + 3

        def inproj(e):
            r = None
            for k in range(8):
                for bank, (c0, c1) in ((bq, (0, 512)), (bk_, (512, 1024)), (bu, (1024, 1280))):
                    r = e.matmul(ps_t[bank][:, 0:c1 - c0], lhsT=xnT[:, k, :], rhs=win_t[:, k, c0:c1],
                                 start=(k == 0), stop=(k == 7))
            return r
        pe(inproj, [xnTb, win_b], [ps_b[bq], ps_b[bk_], ps_b[bu]], cost=3.9)
        qk, qkb = qk_t[par], qk_b[par]
        act(lambda e: e.activation(out=qk[:, 0:512], in_=ps_t[bq][:, 0:512], func=AF.Copy), [ps_b[bq]], [], pw=[qkb])
        act(lambda e: e.activation(out=qk[:, 512:640], in_=ps_t[bk_][:, 0:128], func=AF.Copy), [ps_b[bk_]], [], pw=[qkb])
        act(lambda e: e.activation(out=vu_own[:, 0:384], in_=ps_t[bk_][:, 128:512], func=AF.Copy), [ps_b[bk_]], [], pw=[vu_own_b])
        act(lambda e: e.activation(out=vu_own[:, 384:640], in_=ps_t[bu][:, 0:256], func=AF.Copy), [ps_b[bu]], [], pw=[vu_own_b])
        ri = info["rope"]
        cc = bcast_mid(rope_t[:, ri, 0:16], 10)
        ns = bcast_mid(rope_t[:, ri, 16:24], 10)
        psn = bcast_mid(rope_t[:, ri, 24:32], 10)
        x3 = ps_all[:, bq:bq + 2, :].rearrange("p a c -> p (a c)")[:, 0:640].rearrange("p (h d) -> p h d", h=10)
        qkv3 = qk[:].rearrange("p (h d) -> p h d", h=10)
        tA, tB = rt_t
        dve(lambda e: e.tensor_tensor(out=tA[:], in0=x3[:, :, 0:16], in1=cc, op=ALU.mult), [ps_b[bq], ps_b[bk_], rope_b], [rt_b[0]])
        dve(lambda e: e.tensor_tensor(out=tB[:, :, 0:8], in0=x3[:, :, 8:16], in1=ns, op=ALU.mult), [ps_b[bq], ps_b[bk_], rope_b], [], pw=[rt_b[1]])
        dve(lambda e: e.tensor_tensor(out=tB[:, :, 8:16], in0=x3[:, :, 0:8], in1=psn, op=ALU.mult), [ps_b[bq], ps_b[bk_], rope_b], [], pw=[rt_b[1]])
        dve(lambda e: e.tensor_tensor(out=qkv3[:, :, 0:16], in0=tA[:], in1=tB[:], op=ALU.add), [rt_b[0], rt_b[1]], [qkb])
        is_out = (kind == "sample") or (kind == "prompt" and info["r"] == 15)
        if is_out:
            act(lambda e: e.activation(out=kvf_t[:], in_=ps_t[bk_][:, 0:256], func=AF.Copy), [ps_b[bk_]], [kvf_b])
            act(lambda e: e.activation(out=uf_t[:, 0:256], in_=ps_t[bk_][:, 256:512], func=AF.Copy), [ps_b[bk_]], [], pw=[uf_b])
            act(lambda e: e.activation(out=uf_t[:, 256:512], in_=ps_t[bu][:, 0:256], func=AF.Copy), [ps_b[bu]], [], pw=[uf_b])
            kf3 = kvf_t[:, 0:128].rearrange("p (h d) -> p h d", h=2)
            dve(lambda e: e.tensor_tensor(out=kf3[:, :, 0:16], in0=tA[:, 8:10, :], in1=tB[:, 8:10, :], op=ALU.add),
                [rt_b[0], rt_b[1]], [kvf_b])
            if kind == "prompt":
                sq = info["seq"]
                dmas("sp", [(nkp[l, sq], kvf_t[:, 0:128]), (nvp[l, sq], kvf_t[:, 128:256])], reads=[kvf_b], sembuf=kvf_b)
                dma("sp", npp[l, sq], uf_t[113:128, :], reads=[uf_b], sembuf=uf_b)
            else:
                pairs = []
                for s_ in range(16):
                    pairs.append((nks[l, s_, 120:128, :], kvf_t[8 * s_:8 * s_ + 8, 0:128]))
                    pairs.append((nvs[l, s_, 120:128, :], kvf_t[8 * s_:8 * s_ + 8, 128:256]))
                dmas("sp", pairs, reads=[kvf_b], sembuf=kvf_b)
                dmas("sp", [(nps[l, s_, 7:15, :], uf_t[8 * s_:8 * s_ + 8, :]) for s_ in range(16)], reads=[uf_b], sembuf=uf_b)
        yield
        bqT, bkT, bdT, bpm = B0 + 0, B0 + 1, B0 + 2, B0 + 3
        qTps = ps_t[bqT][:].bitcast(BF16)
        kTps = ps_t[bkT][:].bitcast(BF16)

        def qktr(e):
            r = None
            for hh in range(8):
                r = e.transpose(out=qTps[0:64, hh * 128:(hh + 1) * 128], in_=qk[:, hh * 64:(hh + 1) * 64], identity=ident_t[:])
            for kv in range(2):
                r = e.transpose(out=kTps[0:64, kv * 128:(kv + 1) * 128], in_=qk[:, 512 + kv * 64:512 + (kv + 1) * 64], identity=ident_t[:])
            return r
        pe(qktr, [qkb, ident_b], [ps_b[bqT], ps_b[bkT]], cost=0.8)
        qT, qTb = qT_t[par], qT_b[par]
        if kind != "sample":
            act(lambda e: e.activation(out=qT[:].rearrange("p h t -> p (h t)"), in_=qTps[0:64, :], func=AF.Copy), [ps_b[bqT]], [qTb])
        else:
            for g in range(2):
                src = qTps[0:64, g * 512:(g + 1) * 512].rearrange("p (h s i) -> p h s i", h=4, s=16)
                dst = qT[:].rearrange("p h t -> p (h t)")[:, g * 512:(g + 1) * 512].rearrange("p (s h i) -> p h s i", s=16, h=4)
                dve(lambda e, src=src, dst=dst: e.tensor_copy(out=dst, in_=src), [ps_b[bqT]], [qTb])
        dve(lambda e: e.tensor_copy(out=kT_own[:].rearrange("p h t -> p (h t)"), in_=kTps[0:64, 0:256]), [ps_b[bkT]], [kT_own_b])
        mix, mixb = mix_t[par], mix_b[par]
        dbf, dbfb = d_t[par], d_b[par]
        if kind == "sample":
            for hh in range(2):
                dma("pool", hist_t[0:120, hh, :], stp[l, 8 * hh:8 * hh + 8].rearrange("s r f -> (s r) f"),
                    writes=[hist_b], sembuf=hist_b)

        def poolB(e):
            r = None
            for g in range(4):
                uo = vu_own[:, 128 + g * 128:128 + (g + 1) * 128]
                dst = ps_t[bdT][:, g * 128:(g + 1) * 128]
                if kind == "meta":
                    e.matmul(dst, lhsT=uo, rhs=bmat_t[:, 8 + g, :], start=True, stop=False)
                    r = e.matmul(dst, lhsT=uo, rhs=bmat_t[:, 24 + g, :], start=False, stop=True)
                elif kind == "prompt":
                    up = vu_t[l][prev_slot(ti)][:, 128 + g * 128:128 + (g + 1) * 128]
                    e.matmul(dst, lhsT=up, rhs=bmat_t[:, 4 + g, :], start=True, stop=False)
                    r = e.matmul(dst, lhsT=uo, rhs=bmat_t[:, 0 + g, :], start=False, stop=True)
                else:
                    for hh in range(2):
                        e.matmul(dst, lhsT=hist_t[0:120, hh, g * 128:(g + 1) * 128], rhs=bmat_t[0:120, 16 + 2 * g + hh, :],
                                 start=(hh == 0), stop=False)
                    r = e.matmul(dst, lhsT=uo, rhs=bmat_t[:, 12 + g, :], start=False, stop=True)
            return r
        rd = [vu_own_b, bmat_b]
        if kind == "prompt":
            rd.append(vu_b[l][prev_slot(ti)])
        if kind == "sample":
            rd.append(hist_b)
        pe(poolB, rd, [ps_b[bdT]])
        act(lambda e: e.activation(out=dbf[:].rearrange("p g t -> p (g t)"), in_=ps_t[bdT][:], func=AF.Copy), [ps_b[bdT]], [dbfb])
        yield

        def poolW(e):
            r = None
            for g in range(4):
                r = e.matmul(ps_t[bpm][:, g * 128:(g + 1) * 128], lhsT=pw_t[:, g, :], rhs=dbf[:, g, :], start=True, stop=True)
            return r
        pe(poolW, [dbfb, pw_b], [ps_b[bpm]], cost=0.3)
        dve(lambda e: e.tensor_tensor(out=mix[:, 4:8, :], in0=ps_t[bpm][:].rearrange("p (g t) -> p g t", g=4),
                                      in1=bcast_last(pscale_t[:, l * 4:(l + 1) * 4], 128), op=ALU.mult),
            [ps_b[bpm], pscale_b], [], pw=[mixb])
        yield
        if kind != "sample":
            ps_slot = prev_slot(ti)
            if ps_slot is None:
                kT_prev, kT_prev_b, vu_prev, vu_prev_b = kT_own, kT_own_b, vu_own, vu_own_b
            else:
                kT_prev, kT_prev_b = kT_t[l][ps_slot], kT_b[l][ps_slot]
                vu_prev, vu_prev_b = vu_t[l][ps_slot], vu_b[l][ps_slot]
            P, PT = P_t[par], PT_t[par]
            Pg, PTg, ag = P_bg[par], PT_bg[par], a_bg[par]
            asb = a_t[par]
            sT, sG = stat_g(par)
            sbanks = ((B0 + 0, B0 + 1), (B0 + 2, B0 + 3))
            mask2d = masks_t[:, info["mask"], :]

            def st_scores(g):
                ba, bb_ = sbanks[g]
                Ba = sG[g][0]

                def scores(e):
                    r = None
                    for h4 in range(4):
                        bank = ps_t[ba] if h4 < 2 else ps_t[bb_]
                        o = (h4 % 2) * 256
                        e.matmul(bank[:, o:o + 128], lhsT=qT[:, g * 4 + h4, :], rhs=kT_prev[:, g, :], start=True, stop=False)
                        e.matmul(bank[:, o + 128:o + 256], lhsT=qT[:, g * 4 + h4, :], rhs=kT_own[:, g, :], start=False, stop=False)
                        r = e.matmul(bank[:, o:o + 256], lhsT=ident_t[:], rhs=mask2d, start=False, stop=True)
                    return r
                pe(scores, [qTb, kT_prev_b, kT_own_b, ident_b, masks_b], [ps_b[ba], ps_b[bb_]], cost=1.8)
                for hf, bk in ((0, ba), (1, bb_)):
                    v3 = ps_t[bk][:].rearrange("p (h s) -> p h s", h=2)
                    cc0 = g * 4 + hf * 2
                    dve(lambda e, v3=v3, cc0=cc0: e.tensor_reduce(out=sT[:, cc0:cc0 + 2], in_=v3, axis=AX.X, op=ALU.max),
                        [ps_b[bk]], [], pw=[Ba])
                c0, c1 = g * 4, g * 4 + 4
                dve(lambda e: e.scalar_tensor_tensor(out=sT[:, 8 + c0:8 + c1], in0=sT[:, c0:c1], scalar=SCALE, in1=sinkb_t[:, l * 8 + c0:l * 8 + c1],
                                                     op0=ALU.mult, op1=ALU.max),
                    [Ba, sinkb_b], [Ba], cost=0.2)
                dve(lambda e: e.tensor_scalar(out=sT[:, 16 + c0:16 + c1], in0=sT[:, 8 + c0:8 + c1], scalar1=-1.0, scalar2=None, op0=ALU.mult),
                    [Ba], [Ba], cost=0.2)

            def st_exp(g):
                ba, bb_ = sbanks[g]
                Ba, Bb, Bc = sG[g]
                c0, c1 = g * 4, g * 4 + 4
                for h4 in range(4):
                    bk = ba if h4 < 2 else bb_
                    o = (h4 % 2) * 256
                    hh = g * 4 + h4
                    act(lambda e, bk=bk, o=o, hh=hh: e.activation(out=P[:, hh, :], in_=ps_t[bk][:, o:o + 256], func=AF.Exp,
                                                                   bias=sT[:, 16 + hh:17 + hh], scale=SCALE,
                                                                   accum_out=sT[:, 24 + hh:25 + hh]),
                        [ps_b[bk], Ba], [], pw=[Pg[g], Bb], cost=0.47)
                dve(lambda e: e.tensor_tensor(out=sT[:, 32 + c0:32 + c1], in0=sinkb_t[:, l * 8 + c0:l * 8 + c1], in1=sT[:, 16 + c0:16 + c1], op=ALU.add),
                    [Ba, sinkb_b], [Bc], cost=0.15)
                act(lambda e: e.activation(out=sT[:, 40 + c0:40 + c1], in_=sT[:, 32 + c0:32 + c1], func=AF.Exp), [Bc], [Bc], cost=0.25)
                dve(lambda e: e.tensor_tensor(out=sT[:, 48 + c0:48 + c1], in0=sT[:, 40 + c0:40 + c1], in1=sT[:, 24 + c0:24 + c1], op=ALU.add), [Bc, Bb], [Bc], cost=0.15)
                dve(lambda e: e.reciprocal(out=sT[:, 56 + c0:56 + c1], in_=sT[:, 48 + c0:48 + c1]), [Bc], [Bc], cost=0.2)

            def st_pt(g):
                bpt = sbanks[g][0]
                ptps = ps_t[bpt].bitcast(BF16)

                def ptr(e):
                    r = None
                    for h4 in range(4):
                        for hf in range(2):
                            c = (h4 * 2 + hf) * 128
                            r = e.transpose(out=ptps[:, c:c + 128], in_=P[:, g * 4 + h4, hf * 128:(hf + 1) * 128], identity=ident_t[:])
                    return r
                pe(ptr, [Pg[g], ident_b], [ps_b[bpt]], cost=0.45)
                dst = PT[:, g * 4:(g + 1) * 4, :, :].rearrange("p h f t -> p (h f t)")
                if g == 0:
                    act(lambda e: e.activation(out=dst, in_=ptps[:, :], func=AF.Copy), [ps_b[bpt]], [PTg[g]])
                else:
                    dve(lambda e: e.tensor_copy(out=dst, in_=ptps[:, :]), [ps_b[bpt]], [PTg[g]])

            def st_pv(g):
                bpv = sbanks[g][1]
                Bc = sG[g][2]

                def pv(e):
                    r = None
                    for h4 in range(4):
                        hh = g * 4 + h4
                        dst = ps_t[bpv][:, h4 * 64:(h4 + 1) * 64]
                        e.matmul(dst, lhsT=PT[:, hh, 0, :], rhs=vu_prev[:, g * 64:(g + 1) * 64], start=True, stop=False)
                        r = e.matmul(dst, lhsT=PT[:, hh, 1, :], rhs=vu_own[:, g * 64:(g + 1) * 64], start=False, stop=True)
                    return r
                pe(pv, [PTg[g], vu_prev_b, vu_own_b], [ps_b[bpv]], cost=0.4)
                dve(lambda e: e.tensor_tensor(out=asb[:, g * 256:(g + 1) * 256].rearrange("p (h d) -> p h d", h=4),
                                              in0=ps_t[bpv][:, 0:256].rearrange("p (h d) -> p h d", h=4),
                                              in1=bcast_last(sT[:, 56 + g * 4:60 + g * 4], 64), op=ALU.mult), [ps_b[bpv], Bc], [ag[g]])

            def st_at(g):
                baT = sbanks[g][0]
                aTps = ps_t[baT].bitcast(BF16)

                def atr(e):
                    r = None
                    for c in range(2):
                        cc_ = g * 2 + c
                        r = e.transpose(out=aTps[:, c * 128:(c + 1) * 128], in_=asb[:, cc_ * 128:(cc_ + 1) * 128], identity=ident_t[:])
                    return r
                pe(atr, [ag[g], ident_b], [ps_b[baT]], cost=0.15)
                act(lambda e: e.activation(out=mix[:, g * 2:g * 2 + 2, :].rearrange("p c t -> p (c t)"), in_=aTps[:, 0:256], func=AF.Copy),
                    [ps_b[baT]], [], pw=[mixb])

            st_scores(0)
            yield
            st_scores(1)
            st_exp(0)
            yield
            st_pt(0)
            st_exp(1)
            yield
            st_pv(0)
            st_pt(1)
            yield
            st_at(0)
            st_pv(1)
            yield
            st_at(1)
            yield
        else:
            for T in range(4):
                kp = 0
                dma("pool", kc_t[kp][:], ck[l, 4 * T:4 * T + 4].rearrange("s k f -> k s f"), writes=[kc_b[kp]], sembuf=kc_b[kp])
                dma("pool", vc_t[kp][:], cv[l, 4 * T:4 * T + 4].rearrange("s k f -> k s f"), writes=[vc_b[kp]], sembuf=vc_b[kp])
                bkc = B0 + 0
                kcTps = ps_t[bkc][:].bitcast(BF16)

                def kctr(e, kp=kp, kcTps=kcTps):
                    r = None
                    for j in range(4):
                        for kv in range(2):
                            c = (j * 2 + kv) * 128
                            r = e.transpose(out=kcTps[0:64, c:c + 128], in_=kc_t[kp][:, j, kv * 64:(kv + 1) * 64], identity=ident_t[:])
                    return r
                pe(kctr, [kc_b[kp], ident_b], [ps_b[bkc]])
                act(lambda e, kcTps=kcTps: e.activation(out=kcT_t.rearrange("p j k t -> p (j k t)"), in_=kcTps[0:64, :], func=AF.Copy),
                    [ps_b[bkc]], [kcT_b])
                yield
                for g in range(2):
                    sT, sB, sB2, sB3 = stat(par)
                    pp = g
                    P, Pb = P_t[par], P_bg[par][0]
                    PT, PTb = PT_t[par], PT_bg[par][0]
                    sbk = B0 + 1 + g

                    def sscores(e, T=T, g=g, sbk=sbk):
                        r = None
                        for j in range(4):
                            s = 4 * T + j
                            lt = qT[:].rearrange("p h t -> p (h t)")[:, (g * 16 + s) * 32:(g * 16 + s + 1) * 32]
                            e.matmul(ps_t[sbk][32 * j:32 * j + 32, 0:128], lhsT=lt, rhs=kcT_t[:, j, g, :], start=True, stop=True,
                                     tile_position=(0, 32 * j))
                            r = e.matmul(ps_t[sbk][32 * j:32 * j + 32, 128:256], lhsT=lt, rhs=kT_own[:, g, :], start=True, stop=True,
                                         tile_position=(0, 32 * j))
                        return r
                    pe(sscores, [qTb, kcT_b, kT_own_b], [ps_b[sbk]])
                    sv = ps_t[sbk][:, 0:256]
                    dve(lambda e, sv=sv, T=T: e.tensor_tensor(out=sv, in0=sv, in1=masks_t[:, 3 + T, :], op=ALU.add),
                        [ps_b[sbk], masks_b], [ps_b[sbk]])
                    dve(lambda e, sv=sv, sT=sT: e.tensor_reduce(out=sT[:, 0:1], in_=sv, axis=AX.X, op=ALU.max), [ps_b[sbk]], [sB])
                    sc = sinkc_t[:, l * 2 + g:l * 2 + g + 1]
                    dve(lambda e, sT=sT, sc=sc: e.scalar_tensor_tensor(out=sT[:, 1:2], in0=sT[:, 0:1], scalar=SCALE, in1=sc,
                                                                       op0=ALU.mult, op1=ALU.max), [sB, sinkc_b], [sB])
                    dve(lambda e, sT=sT: e.tensor_scalar(out=sT[:, 2:3], in0=sT[:, 1:2], scalar1=-1.0, scalar2=None, op0=ALU.mult), [sB], [sB])
                    yield
                    act(lambda e, sv=sv, sT=sT, P=P, g=g: e.activation(out=P[:, g, :], in_=sv, func=AF.Exp, bias=sT[:, 2:3], scale=SCALE,
                                                                       accum_out=sT[:, 3:4]), [ps_b[sbk], sB], [sB2], pw=[Pb])
                    dve(lambda e, sT=sT, sc=sc: e.tensor_tensor(out=sT[:, 4:5], in0=sc, in1=sT[:, 2:3], op=ALU.add), [sB, sinkc_b], [sB3])
                    act(lambda e, sT=sT: e.activation(out=sT[:, 5:6], in_=sT[:, 4:5], func=AF.Exp), [sB3], [sB3])
                    dve(lambda e, sT=sT: e.tensor_tensor(out=sT[:, 6:7], in0=sT[:, 5:6], in1=sT[:, 3:4], op=ALU.add), [sB3, sB2], [sB3])
                    dve(lambda e, sT=sT: e.reciprocal(out=sT[:, 7:8], in_=sT[:, 6:7]), [sB3], [sB3])
                    yield
                    bpt = B0 + 3
                    ptps = ps_t[bpt][:].bitcast(BF16)

                    def sptr(e, P=P, ptps=ptps, g=g):
                        e.transpose(out=ptps[:, 0:128], in_=P[:, g, 0:128], identity=ident_t[:])
                        return e.transpose(out=ptps[:, 128:256], in_=P[:, g, 128:256], identity=ident_t[:])
                    pe(sptr, [Pb, ident_b], [ps_b[bpt]])
                    act(lambda e, PT=PT, ptps=ptps, g=g: e.activation(out=PT[:, g, :, :].rearrange("p f t -> p (f t)"), in_=ptps[:, 0:256], func=AF.Copy),
                        [ps_b[bpt]], [], pw=[PTb])
                    yield
                    bsv = B0 + 0

                    def spv(e, PT=PT, g=g, kp=kp, bsv=bsv):
                        r = None
                        for j in range(4):
                            dst = ps_t[bsv][32 * j:32 * j + 32, 0:64]
                            e.matmul(dst, lhsT=PT[:, g, 0, 32 * j:32 * j + 32], rhs=vc_t[kp][:, j, g * 64:(g + 1) * 64], start=True, stop=False,
                                     tile_position=(0, 32 * j))
                            r = e.matmul(dst, lhsT=PT[:, g, 1, 32 * j:32 * j + 32], rhs=vu_own[:, g * 64:(g + 1) * 64], start=False, stop=True,
                                         tile_position=(0, 32 * j))
                        return r
                    pe(spv, [PTb, vc_b[kp], vu_own_b], [ps_b[bsv]])
                    asb, asbb = a_t[par], a_bg[par][0]
                    dve(lambda e, asb=asb, sT=sT, g=g, bsv=bsv: e.tensor_scalar(out=asb[:, g * 64:(g + 1) * 64], in0=ps_t[bsv][:, 0:64], scalar1=sT[:, 7:8], scalar2=None, op0=ALU.mult),
                        [ps_b[bsv], sB3], [], pw=[asbb])
                    yield
                    aTs = ps_t[bpt][:].bitcast(BF16)

                    def satr(e, asb=asb, aTs=aTs, g=g):
                        e.transpose(out=aTs[0:64, 0:128], in_=asb[:, g * 64:(g + 1) * 64], identity=ident_t[:])
                        return e.transpose(out=aTs[64:128, 0:128], in_=asb[:, g * 64:(g + 1) * 64], identity=ident_t[:], tile_position=(0, 64))
                    pe(satr, [asbb, ident_b], [ps_b[bpt]])
                    for h4 in range(4):
                        p0 = 0 if h4 % 2 == 0 else 64
                        c = g * 2 + h4 // 2
                        src = aTs[p0:p0 + 64, 0:128].rearrange("p (j h i) -> p j h i", j=4, h=4)[:, :, h4, :]
                        dst = mix[p0:p0 + 64, c, 32 * T:32 * T + 32].rearrange("p (j i) -> p j i", j=4)
                        if h4 < 2:
                            act(lambda e, src=src, dst=dst: e.activation(out=dst, in_=src, func=AF.Copy), [ps_b[bpt]], [], pw=[mixb])
                        else:
                            dve(lambda e, src=src, dst=dst: e.tensor_copy(out=dst, in_=src), [ps_b[bpt]], [], pw=[mixb])
                    yield
        bo = B0

        def oproj(e):
            r = None
            for c in range(8):
                for hf in range(2):
                    r = e.matmul(ps_t[bo + hf][:], lhsT=mix[:, c, :], rhs=wout_t[:, c, hf * 512:(hf + 1) * 512], start=(c == 0), stop=(c == 7))
            return r
        pe(oproj, [mixb, wout_b], [ps_b[bo], ps_b[bo + 1]], cost=3.6)
        for hf in range(2):
            dve(lambda e, hf=hf: e.tensor_tensor(out=h_t[hs][:, hf * 512:(hf + 1) * 512], in0=h_t[hs][:, hf * 512:(hf + 1) * 512],
                                                 in1=ps_t[bo + hf][:], op=ALU.add), [h_b[hs], ps_b[bo + hf]], [h_b[hs]])
        yield
        norm_T(hs, 1, xn2T_t[:, :, hs * 128:(hs + 1) * 128], xn2T_bt[hs], par, B0 + 2,
               xh=P_t[par][:, 0:4, :].rearrange("p h s -> p (h s)"), xhb=P_bg[par][0])

    def att_phase(l, tiles):
        LAG = 3
        gens = [att_tile(l, ti, hs, hs % 2) for hs, ti in enumerate(tiles)]
        inflight, progress, nxt = [], {}, 0
        while nxt < len(gens) or inflight:
            if nxt < len(gens) and len(inflight) < 2 and (not inflight or progress[inflight[-1]] >= LAG):
                inflight.append(nxt)
                progress[nxt] = 0
                nxt += 1
            for g in list(inflight):
                try:
                    next(gens[g])
                    progress[g] += 1
                except StopIteration:
                    inflight.remove(g)

    def mlp_phase(l, ntiles):
        nsub = (ntiles + 3) // 4
        base, rem = divmod(ntiles, nsub)
        subs, a0 = [], 0
        for i_ in range(nsub):
            a1 = a0 + base + (1 if i_ < rem else 0)
            subs.append((a0, a1))
            a0 = a1
        upc = [0]
        for p in range(NPIECE):
            s = p % 2
            for (t0, t1) in subs:
                n = (t1 - t0) * 128
                ai = upc[0] % 2
                upc[0] += 1
                aT, aTb = aT_t[ai], aT_b[ai]
                pend = None
                for fc in range(4):
                    bk = fc % 4

                    def up(e, fc=fc, bk=bk, t0=t0, n=n, s=s):
                        r = None
                        for k in range(8):
                            r = e.matmul(ps_t[bk][:, 0:n], lhsT=wup_t[s][:, k, fc * 128:(fc + 1) * 128],
                                         rhs=xn2T_t[:, k, t0 * 128:t0 * 128 + n], start=(k == 0), stop=(k == 7))
                        return r
                    pe(up, [wup_b[s]] + [xn2T_bt[t_] for t_ in range(t0, t1)], [ps_b[bk]], cost=1.75 * n / 512.0)
                    ri = fc % 2
                    act(lambda e, bk=bk, n=n, ri=ri: e.activation(out=r_t[ri][:, 0:n], in_=ps_t[bk][:, 0:n], func=AF.Relu),
                        [ps_b[bk]], [r_b[ri]], cost=0.55)
                    if pend is not None:
                        pend()
                    pend = (lambda ri=ri, n=n, fc=fc, aT=aT, aTb=aTb:
                            act(lambda e: e.activation(out=aT[:, fc, 0:n], in_=r_t[ri][:, 0:n], func=AF.Square), [r_b[ri]], [], pw=[aTb], cost=0.45))
                pend()
                for t in range(t0, t1):
                    db = 4 + 2 * (t % 2)

                    def dn(e, t=t, t0=t0, db=db, aT=aT, s=s):
                        r = None
                        for fc in range(4):
                            for hf in range(2):
                                r = e.matmul(ps_t[db + hf][:], lhsT=aT[:, fc, (t - t0) * 128:(t - t0 + 1) * 128],
                                             rhs=wdn_t[s][:, fc, hf * 512:(hf + 1) * 512], start=(fc == 0), stop=(fc == 3))
                        return r
                    pe(dn, [aTb, wdn_b[s]], [ps_b[db], ps_b[db + 1]], cost=1.75)
                    for hf in range(2):
                        dve(lambda e, t=t, hf=hf, db=db: e.tensor_tensor(out=h_t[t][:, hf * 512:(hf + 1) * 512], in0=h_t[t][:, hf * 512:(hf + 1) * 512],
                                                                          in1=ps_t[db + hf][:], op=ALU.add), [h_b[t], ps_b[db + hf]], [h_b[t]])
            if p + 2 < NPIECE:
                load_mlp_piece(l, p + 2)

    def final_tile(ti, hs, last_mt=False):
        info = tile_info(ti)
        if info["kind"] == "meta":
            return
        par = hs % 2
        sT, sB, _, _ = stat(par)
        xh, xhb = xhat_t[par], xhat_b[par]
        act(lambda e: e.activation(out=xh[:], in_=h_t[hs][:], func=AF.Square, accum_out=sT[:, 0:1]), [h_b[hs]], [xhb, sB])
        act(lambda e: e.activation(out=sT[:, 1:2], in_=sT[:, 0:1], func=AF.Ln, bias=eps_t[:, 0:1], scale=1.0 / D), [sB, eps_b], [sB])
        act(lambda e: e.activation(out=sT[:, 2:3], in_=sT[:, 1:2], func=AF.Exp, scale=-0.5), [sB], [sB])
        if info["kind"] == "prompt":
            dst = y_p[info["seq"], info["r"] * 128:(info["r"] + 1) * 128, :]
        else:
            dst = y_s
        if last_mt:
            dve(lambda e: e.scalar_tensor_tensor(out=h_t[hs][:], in0=h_t[hs][:], scalar=sT[:, 2:3], in1=lnf_t[:], op0=ALU.mult, op1=ALU.mult),
                [h_b[hs], sB, lnf_b], [h_b[hs]], cost=1.2)
            dma("sp", dst, h_t[hs][:], reads=[h_b[hs]], sembuf=h_b[hs])
        else:
            ys = aT_t[par][:].rearrange("p a c -> p (a c)").bitcast(F32)
            dve(lambda e: e.scalar_tensor_tensor(out=ys, in0=h_t[hs][:], scalar=sT[:, 2:3], in1=lnf_t[:], op0=ALU.mult, op1=ALU.mult),
                [h_b[hs], sB, lnf_b], [aT_b[par]], cost=1.2)
            dma("sp", dst, ys, reads=[aT_b[par]], sembuf=aT_b[par])

    load_att_weights(0)
    for mi, tiles in enumerate(MTS):
        for hs, ti in enumerate(tiles):
            info = tile_info(ti)
            if info["kind"] == "meta":
                dve(lambda e, hs=hs: e.memset(h_t[hs][:], 0.0), [], [h_b[hs]])
                dma("sp", h_t[hs][112:128, :], meta, writes=[h_b[hs]], sembuf=h_b[hs])
            elif info["kind"] == "prompt":
                dma("sp", h_t[hs][:], x_p[info["seq"], info["r"] * 128:(info["r"] + 1) * 128, :], writes=[h_b[hs]], sembuf=h_b[hs])
            else:
                dma("sp", h_t[hs][:], x_s, writes=[h_b[hs]], sembuf=h_b[hs])
        for l in range(L):
            S.epoch += 1
            load_mlp_piece(l, 0)
            load_mlp_piece(l, 1)
            for gi in range(2):
                dma("sp", g_t[gi][:], gbc[l, gi], writes=[g_b[gi]], sembuf=g_b[gi])
            att_phase(l, tiles)
            nl, nm = (l + 1, mi) if l + 1 < L else (0, mi + 1)
            if nm < len(MTS):
                load_att_weights(nl)
            mlp_phase(l, len(tiles))
        for hs, ti in enumerate(tiles):
            final_tile(ti, hs, last_mt=(mi == len(MTS) - 1))

    if LIMIT is not None:
        S.ops = S.ops[:LIMIT]
    def dummy_mm(e, bank):
        e.matmul(ps_t[bank][:, 0:512], lhsT=ident_t[:], rhs=bmat_t[:, 0:4, :].rearrange("p a t -> p (a t)"), start=True, stop=True)
    nsem = lower(nc, S, stack, allbufs, reorder=REORDER, dummy=(dummy_mm if DUMMY else None),
                 dummy_deps=list(ident_b.writers) + list(bmat_b.writers))
    stack.close()
    return nc, nsem, len(S.ops)


_CACHE = {}


def kernel(x_prompt, x_sample, cache_k, cache_v, state_pool, meta_tokens, ln1, w_in, attn_sinks,
           pool_w, pool_scale, w_out, ln2, w_up, w_down, ln_f):
    f = lambda a: np.ascontiguousarray(np.asarray(a, dtype=np.float32))
    x_prompt, x_sample, cache_k, cache_v, state_pool = map(f, (x_prompt, x_sample, cache_k, cache_v, state_pool))
    meta_tokens, ln1, w_in, attn_sinks, pool_w, pool_scale, w_out, ln2, w_up, w_down, ln_f = map(
        f, (meta_tokens, ln1, w_in, attn_sinks, pool_w, pool_scale, w_out, ln2, w_up, w_down, ln_f))
    if "nc" not in _CACHE:
        _CACHE["nc"] = build_program()[0]
        _CACHE["consts"] = make_consts()
    nc = _CACHE["nc"]
    consts = _CACHE["consts"]
    gbc = np.broadcast_to(np.stack([ln1, ln2], 1).reshape(L, 2, 1, D), (L, 2, 128, D))
    psc = pool_scale.reshape(L, 4, 128).transpose(2, 0, 1).reshape(128, L * 4)
    skb = np.broadcast_to(attn_sinks.reshape(1, L * 8), (128, L * 8))
    rows = np.arange(128)
    h4 = (rows // 8) % 4
    skc = np.stack([attn_sinks[l, kv * 4 + h4] for l in range(L) for kv in range(2)], axis=1)
    lnfb = np.broadcast_to(ln_f.reshape(1, D), (128, D))
    shared = {
        "meta": meta_tokens, "w_in": w_in, "w_out": w_out, "pool_w": pool_w, "w_up": w_up, "w_dn": w_down,
        "gbc": f(gbc), "pscale": f(psc), "sinkb": f(skb), "sinkc": f(skc), "lnf": f(lnfb),
        "ident": consts["ident"], "masks": consts["masks"], "bmat": consts["bmat"], "rope": consts["rope"],
    }
    in_maps = []
    for c in range(NCORES):
        m = dict(shared)
        m["x_p"] = x_prompt[2 * c:2 * c + 2]
        m["x_s"] = x_sample[16 * c:16 * c + 16].reshape(128, D)
        m["ck"] = f(cache_k[:, 16 * c:16 * c + 16].reshape(L, 16, 128, 128))
        m["cv"] = f(cache_v[:, 16 * c:16 * c + 16].reshape(L, 16, 128, 128))
        m["stp"] = f(state_pool[:, 16 * c:16 * c + 16])
        in_maps.append(m)
    res = run_bass_kernel_spmd(nc, in_maps, core_ids=list(range(NCORES)))
    R = res.results
    y_prompt = np.concatenate([R[c]["y_p"] for c in range(NCORES)], 0)
    y_sample = np.concatenate([R[c]["y_s"].reshape(16, 8, D) for c in range(NCORES)], 0)
    nkp = np.concatenate([R[c]["nkp"] for c in range(NCORES)], 1).reshape(L, 16, 128, 2, 64)
    nvp = np.concatenate([R[c]["nvp"] for c in range(NCORES)], 1).reshape(L, 16, 128, 2, 64)
    npp = np.concatenate([R[c]["npp"] for c in range(NCORES)], 1)
    nks = np.concatenate([R[c]["nks"] for c in range(NCORES)], 1).reshape(L, 128, 128, 2, 64)
    nvs = np.concatenate([R[c]["nvs"] for c in range(NCORES)], 1).reshape(L, 128, 128, 2, 64)
    nps = np.concatenate([R[c]["nps"] for c in range(NCORES)], 1)
    return (y_prompt.astype(np.float32), y_sample.astype(np.float32), nkp, nvp, npp, nks, nvs, nps)
```
